# Optimizing a Trainium2 kernel written in Bass

```python
import math
import jax, jax.numpy as jnp
from jax import lax
import numpy as np

D_MODEL = 2048
BATCH = 8
SEQ = 2048
DEPTH = 1

MIX_WIDTH = D_MODEL
ATTN_WIDTH = MIX_WIDTH // 2
HYENA_WIDTH = MIX_WIDTH - ATTN_WIDTH
N_HEADS = 8
HEAD_DIM = ATTN_WIDTH // N_HEADS // 2
V_HEAD_DIM = 2 * HEAD_DIM
Q_BLOCK = 128
SHORT_CONV = 3
FILTER_EMB = 33
FILTER_HIDDEN = 64
DECAY_FAST = 0.3
DECAY_SLOW = 1.5
DECAY_TARGET = 1e-2
DECAY_SHIFT = 0.0
D_FF = 4 * D_MODEL
ALPHA = (2.0 * DEPTH) ** 0.25
BETA = (8.0 * DEPTH) ** -0.25
EPS = 1e-5
IN_COLS = 3 * ATTN_WIDTH + 3 * HYENA_WIDTH

kernel_name = "hymba_diffattn_hyena_deepnorm_encoder"


def layer_norm(x, g, b):
    xf = x.astype(jnp.float32)
    mu = jnp.mean(xf, axis=-1, keepdims=True)
    var = jnp.mean(jnp.square(xf - mu), axis=-1, keepdims=True)
    y = (xf - mu) * lax.rsqrt(var + EPS) * g.astype(jnp.float32) + b.astype(jnp.float32)
    return y.astype(x.dtype)


def rms_norm(x, g):
    xf = x.astype(jnp.float32)
    y = xf * lax.rsqrt(jnp.mean(jnp.square(xf), axis=-1, keepdims=True) + EPS)
    return y * g.astype(jnp.float32)


def alibi_slopes(n_heads):
    return jnp.asarray(np.array([2.0 ** (-8.0 * (h + 1) / n_heads) for h in range(n_heads)], dtype=np.float32))


def diff_attention(q, k, v, lam, slopes):
    B, S = q.shape[0], q.shape[1]
    nb = S // Q_BLOCK
    scale = HEAD_DIM ** -0.5
    qb = jnp.moveaxis(q.reshape(B, nb, Q_BLOCK, N_HEADS, 2, HEAD_DIM), 1, 0)
    starts = jnp.arange(nb, dtype=jnp.int32) * Q_BLOCK
    key_pos = jnp.arange(S, dtype=jnp.int32)

    def block(args):
        q_i, start = args
        s = jnp.einsum("bqhcd,bkhcd->bhcqk", q_i, k, preferred_element_type=jnp.float32) * scale
        q_pos = start + jnp.arange(Q_BLOCK, dtype=jnp.int32)
        dist = jnp.abs(q_pos[:, None] - key_pos[None, :]).astype(jnp.float32)
        s = s - slopes[None, :, None, None, None] * dist[None, None, None]
        p = jax.nn.softmax(s, axis=-1)
        a = p[:, :, 0] - lam * p[:, :, 1]
        return jnp.einsum("bhqk,bkhe->bqhe", a.astype(v.dtype), v, preferred_element_type=jnp.float32)

    o = lax.map(block, (qb, starts))
    return jnp.moveaxis(o, 0, 1).reshape(B, S, N_HEADS, V_HEAD_DIM)


def hyena_filters(L, w1, b1, freq, w2, b2, w3):
    f32 = jnp.float32
    t = jnp.linspace(0.0, 1.0, L, dtype=f32)[:, None]
    bands = (FILTER_EMB - 1) // 2
    w = 2.0 * math.pi * jnp.arange(L, dtype=f32)[:, None] / L
    f = jnp.linspace(1e-4, bands - 1, bands, dtype=f32)[None, :]
    z = jnp.concatenate([t, jnp.cos(f * w), -jnp.sin(f * w)], axis=-1)
    fr = freq.astype(f32)
    h = jnp.sin(fr * (z @ w1.astype(f32) + b1.astype(f32)))
    h = jnp.sin(fr * (h @ w2.astype(f32) + b2.astype(f32)))
    h = h @ w3.astype(f32)
    max_decay = math.log(DECAY_TARGET) / DECAY_FAST
    min_decay = math.log(DECAY_TARGET) / DECAY_SLOW
    deltas = jnp.linspace(min_decay, max_decay, HYENA_WIDTH, dtype=f32)
    decay = jnp.exp(-t * jnp.abs(deltas)[None, :])
    h = h * (jnp.tile(decay, (1, 2)) + DECAY_SHIFT)
    return h[:, :HYENA_WIDTH], h[:, HYENA_WIDTH:]


def hyena_mixer(u, conv_w, conv_b, w1, b1, freq, w2, b2, w3, d_skip):
    L = u.shape[1]
    pad = SHORT_CONV // 2
    up = jnp.pad(u, ((0, 0), (pad, pad), (0, 0)))
    z = conv_b
    for j in range(SHORT_CONV):
        z = z + up[:, j:j + L] * conv_w[j]
    x1, x2, v = jnp.split(z, 3, axis=-1)
    hf, hb = hyena_filters(L, w1, b1, freq, w2, b2, w3)
    kern = jnp.concatenate([hf, jnp.zeros((1, HYENA_WIDTH), jnp.float32), hb[1:][::-1]], axis=0)
    n = 2 * L
    vg = (v * x2).astype(jnp.float32)
    y = jnp.fft.irfft(jnp.fft.rfft(vg, n=n, axis=1) * jnp.fft.rfft(kern, n=n, axis=0)[None], n=n, axis=1)[:, :L]
    y = y + vg * d_skip.astype(jnp.float32)
    return y * x1.astype(jnp.float32)


def setup_inputs(seed: int = 0) -> dict:
    key = jax.random.key(seed)
    ks = jax.random.split(key, 26)
    nrm = lambda k, s: jax.random.normal(k, s, dtype=jnp.float32)
    col_scale = np.ones((IN_COLS,), dtype=np.float32)
    col_scale[2 * ATTN_WIDTH:3 * ATTN_WIDTH] = BETA
    col_scale[3 * ATTN_WIDTH + 2 * HYENA_WIDTH:] = BETA
    return {
        "x": nrm(ks[0], (BATCH, SEQ, D_MODEL)),
        "w_in": nrm(ks[1], (DEPTH, D_MODEL, IN_COLS)) * (D_MODEL ** -0.5) * jnp.asarray(col_scale),
        "lambda_q1": nrm(ks[2], (DEPTH, HEAD_DIM)) * 0.1,
        "lambda_k1": nrm(ks[3], (DEPTH, HEAD_DIM)) * 0.1,
        "lambda_q2": nrm(ks[4], (DEPTH, HEAD_DIM)) * 0.1,
        "lambda_k2": nrm(ks[5], (DEPTH, HEAD_DIM)) * 0.1,
        "subln_g": 1.0 + 0.02 * nrm(ks[6], (DEPTH, V_HEAD_DIM)),
        "conv_w": nrm(ks[7], (DEPTH, SHORT_CONV, 3 * HYENA_WIDTH)) * (SHORT_CONV ** -0.5),
        "conv_b": 0.02 * nrm(ks[8], (DEPTH, 3 * HYENA_WIDTH)),
        "filt_w1": nrm(ks[9], (DEPTH, FILTER_EMB, FILTER_HIDDEN)) * (FILTER_EMB ** -0.5),
        "filt_b1": 0.1 * nrm(ks[10], (DEPTH, FILTER_HIDDEN)),
        "filt_freq": 1.0 + 0.01 * nrm(ks[11], (DEPTH, FILTER_HIDDEN)),
        "filt_w2": nrm(ks[12], (DEPTH, FILTER_HIDDEN, FILTER_HIDDEN)) * (FILTER_HIDDEN ** -0.5),
        "filt_b2": 0.1 * nrm(ks[13], (DEPTH, FILTER_HIDDEN)),
        "filt_w3": nrm(ks[14], (DEPTH, FILTER_HIDDEN, 2 * HYENA_WIDTH)) * (FILTER_HIDDEN ** -0.5) * 0.1,
        "hyena_skip": nrm(ks[15], (DEPTH, HYENA_WIDTH)),
        "hyena_gain": 1.0 + 0.02 * nrm(ks[16], (DEPTH, HYENA_WIDTH)),
        "w_out": nrm(ks[17], (DEPTH, MIX_WIDTH, D_MODEL)) * (MIX_WIDTH ** -0.5) * BETA,
        "ln1_g": 1.0 + 0.02 * nrm(ks[18], (DEPTH, D_MODEL)),
        "ln1_b": 0.02 * nrm(ks[19], (DEPTH, D_MODEL)),
        "w_ff1": nrm(ks[20], (DEPTH, D_MODEL, D_FF)) * (D_MODEL ** -0.5) * BETA,
        "w_ff2": nrm(ks[21], (DEPTH, D_FF, D_MODEL)) * (D_FF ** -0.5) * BETA,
        "ln2_g": 1.0 + 0.02 * nrm(ks[22], (DEPTH, D_MODEL)),
        "ln2_b": 0.02 * nrm(ks[23], (DEPTH, D_MODEL)),
    }


def reference(x, w_in, lambda_q1, lambda_k1, lambda_q2, lambda_k2, subln_g, conv_w, conv_b,
              filt_w1, filt_b1, filt_freq, filt_w2, filt_b2, filt_w3, hyena_skip, hyena_gain,
              w_out, ln1_g, ln1_b, w_ff1, w_ff2, ln2_g, ln2_b):
    B, S, _ = x.shape
    A = ATTN_WIDTH
    slopes = alibi_slopes(N_HEADS)
    for l in range(DEPTH):
        lam_init = 0.8 - 0.6 * math.exp(-0.3 * l)
        proj = jnp.einsum("bsd,dn->bsn", x, w_in[l])
        q = proj[..., :A].reshape(B, S, N_HEADS, 2, HEAD_DIM)
        k = proj[..., A:2 * A].reshape(B, S, N_HEADS, 2, HEAD_DIM)
        v = proj[..., 2 * A:3 * A].reshape(B, S, N_HEADS, V_HEAD_DIM)
        lam = (jnp.exp(jnp.sum(lambda_q1[l].astype(jnp.float32) * lambda_k1[l].astype(jnp.float32)))
               - jnp.exp(jnp.sum(lambda_q2[l].astype(jnp.float32) * lambda_k2[l].astype(jnp.float32)))
               + lam_init)
        att = diff_attention(q, k, v, lam, slopes)
        att = (rms_norm(att, subln_g[l]) * (1.0 - lam_init)).reshape(B, S, A).astype(x.dtype)
        hy = hyena_mixer(proj[..., 3 * A:], conv_w[l], conv_b[l], filt_w1[l], filt_b1[l], filt_freq[l],
                         filt_w2[l], filt_b2[l], filt_w3[l], hyena_skip[l])
        hy = rms_norm(hy, hyena_gain[l]).astype(x.dtype)
        mix = jnp.einsum("bsm,md->bsd", jnp.concatenate([att, hy], axis=-1), w_out[l])
        x = layer_norm(ALPHA * x + mix, ln1_g[l], ln1_b[l])
        h = jnp.square(jax.nn.relu(jnp.einsum("bsd,df->bsf", x, w_ff1[l])))
        x = layer_norm(ALPHA * x + jnp.einsum("bsf,fd->bsd", h, w_ff2[l]), ln2_g[l], ln2_b[l])
    return x
```

```python
import math
import os
from contextlib import ExitStack
import numpy as np
import ml_dtypes
import concourse.bass as bass
import concourse.mybir as mybir
from concourse.bass_utils import run_bass_kernel_spmd

F32 = mybir.dt.float32
BF16 = mybir.dt.bfloat16
ALU = mybir.AluOpType
AF = mybir.ActivationFunctionType

ENGS = ("pe", "act", "dve", "pool", "sp")

D = 2048
S = 2048
A = 1024
C = 1024
NH = 8
DFF = 8192
L = S
NFFT = 2 * L
EPS = 1e-5
ALPHA = 2.0 ** 0.25
LAM_INIT = 0.8 - 0.6 * math.exp(0.0)
SCALE = 64 ** -0.5
PI = math.pi


class T:
    __slots__ = ("w", "r")

    def __init__(self):
        self.w = None
        self.r = {}


class Prog:
    def __init__(self):
        self.ops = {e: [] for e in ENGS}
        self.cnt = {e: 0 for e in ENGS}
        self.waited = {e: {} for e in ENGS}
        self.dma_cnt = {}

    def _wait(self, eng, tok):
        if tok is None:
            return
        key, val = tok
        if eng == "pe" and key == "pe":
            return
        if val > self.waited[eng].get(key, 0):
            self.waited[eng][key] = val
            self.ops[eng].append(("wait", key, val))

    def new_dma_sem(self, name):
        self.dma_cnt[name] = 0
        return name

    def _deps(self, eng, reads, writes):
        for t in reads:
            self._wait(eng, t.w)
        for t in writes:
            self._wait(eng, t.w)
            for k, v in t.r.items():
                self._wait(eng, (k, v))

    def _mark(self, tok, reads, writes):
        for t in reads:
            if t.r.get(tok[0], 0) < tok[1]:
                t.r[tok[0]] = tok[1]
        for t in writes:
            t.w = tok
            t.r = {}

    def op(self, eng, fn, reads=(), writes=(), signal=True):
        self._deps(eng, reads, writes)
        if signal:
            self.cnt[eng] += 1
            tok = (eng, self.cnt[eng])
        else:
            tok = (eng, self.cnt[eng] + 1)
        self.ops[eng].append(("op", fn, signal))
        self._mark(tok, reads, writes)
        return tok

    def dma(self, eng, sem, fn, reads=(), writes=(), n=1):
        self._deps(eng, reads, writes)
        self.dma_cnt[sem] += 16 * n
        tok = (sem, self.dma_cnt[sem])
        self.ops[eng].append(("dma", fn, sem))
        self._mark(tok, reads, writes)
        return tok

    def barrier(self):
        for e in ENGS:
            for f in ENGS:
                if f != e and self.cnt[f] > 0:
                    self._wait(e, (f, self.cnt[f]))
            for s, v in self.dma_cnt.items():
                if v > 0:
                    self._wait(e, (s, v))

    def replay(self, block, sems):
        engmap = {"pe": "tensor", "act": "scalar", "dve": "vector", "pool": "gpsimd", "sp": "sync"}
        prog = self

        def make(e):
            def body(engine):
                for item in prog.ops[e]:
                    if item[0] == "wait":
                        engine.wait_ge(sems[item[1]], item[2])
                    elif item[0] == "op":
                        ins = item[1](engine)
                        if item[2]:
                            ins.then_inc(sems[e], 1)
                    else:
                        res = item[1](engine)
                        if not isinstance(res, (list, tuple)):
                            res = [res]
                        for ins in res:
                            ins.then_inc(sems[item[2]], 16)
            return body

        for e in ENGS:
            getattr(block, engmap[e])(make(e))


def MM(out, lhsT, rhs, start, stop):
    return lambda e: e.matmul(out, lhsT, rhs, start=start, stop=stop)


def ACTF(out, in_, func, **kw):
    return lambda e: e.activation(out=out, in_=in_, func=func, **kw)


def TT(out, in0, in1, op):
    return lambda e: e.tensor_tensor(out=out, in0=in0, in1=in1, op=op)


def TS(out, in0, s1, s2, op0, op1=None):
    if op1 is None:
        return lambda e: e.tensor_scalar(out=out, in0=in0, scalar1=s1, scalar2=None, op0=op0)
    return lambda e: e.tensor_scalar(out=out, in0=in0, scalar1=s1, scalar2=s2, op0=op0, op1=op1)


def STT(out, in0, scalar, in1, op0, op1):
    return lambda e: e.scalar_tensor_tensor(out=out, in0=in0, scalar=scalar, in1=in1, op0=op0, op1=op1)


SMO = {}
_o = 0
for _n, _w in (("ident", 64), ("ones", 64), ("cw", 96), ("fw1", 64), ("fw2", 64), ("fvec", 8), ("tl", 16), ("hyg", 8),
               ("lamin", 256), ("lamw", 8), ("subg", 2), ("lnv", 64), ("lnva", 64), ("lamtmp", 64), ("epsc", 4)):
    SMO[_n] = (_o, _w)
    _o += _w
SM_USED = _o
ARENA = 52000
SM_WORDS = 3000
R1 = SM_WORDS
R2 = R1 + 16384
R3 = R2 + 8192
R4 = R3 + 8192
R4_WORDS = ARENA - R4


def build_nc(stop_after=None, dbg_words=0):
    nc = bass.Bass("TRN2", target_bir_lowering=False)
    dr = {}

    order = ["A", "B", "C", "D", None]
    lvl = order.index(stop_after)

    def din(name, shape, dt=F32, need="A"):
        if order.index(need) > lvl:
            return None
        dr[name] = nc.dram_tensor(name, list(shape), dt, kind="ExternalInput").ap()
        return dr[name]

    xT_d = din("xT", [16, 128, 2048])
    wh_d = din("wh", [8, 128, 16 * 384])
    wqkv_d = din("wqkv", [8, 128, 16 * 384], need="D")
    wo_d = din("wo", [4, 128, 16 * 512], need=None)
    w1_d = din("w1", [16, 128, 16 * 512], need=None)
    w2_d = din("w2", [16, 128, 4 * 2048], need=None)
    ff_d = din("dft_f", [16, 128, 2 * 16 * 128], BF16, need="C")
    fi_d = din("dft_i", [8, 128, 2 * 16 * 256], BF16, need="C")
    dist_d = din("dist", [128, 3968], need="D")
    smalls_d = din("smalls", [128, SM_USED])
    csm_d = din("csm", [128, 4096], need="C")
    zT_d = din("zTp", [128, 2048], need="C")
    yT_d = nc.dram_tensor("yT", [16, 128, 2048], F32, kind="ExternalOutput").ap()
    dbg_d = None
    if dbg_words:
        dbg_d = nc.dram_tensor("dbg", [128, dbg_words], F32, kind="ExternalOutput").ap()

    nc._declared_inputs = list(dr.keys())
    P = Prog()
    es = ExitStack()
    with es:
        ar = es.enter_context(nc.sbuf_tensor("arena", [128, ARENA], F32))
        ps = es.enter_context(nc.psum_tensor("ps", [128, 4096], F32))
        sems = {e: es.enter_context(nc.semaphore("s_" + e)) for e in ENGS}

        def dsem(name):
            P.new_dma_sem(name)
            sems[name] = es.enter_context(nc.semaphore("d_" + name))
            return name

        def f32v(off, n, parts=128):
            return ar[0:parts, off:off + n]

        def bfv(off, nwords, parts=128):
            return ar[0:parts, off:off + nwords].bitcast(BF16)

        def bank(i, n=512):
            return ps[:, i * 512:i * 512 + n]

        def bankbf(i):
            return ps[:, i * 512:(i + 1) * 512].bitcast(BF16)

        TB = [T() for _ in range(8)]
        bank_rr = [0]

        def nextbank(pool):
            b = pool[bank_rr[0] % len(pool)]
            bank_rr[0] += 1
            return b

        def dump_and_finish(regions):
            P.barrier()
            s_dbg = dsem("dbg")
            off = [0]
            outs = []
            for (o, n) in regions:
                outs.append((dbg_d[:, off[0]:off[0] + n], ar[:, o:o + n]))
                off[0] += n
            tk = P.dma("sp", s_dbg, lambda g: [g.dma_start(out=a, in_=b) for (a, b) in outs], n=len(outs))
            P._wait("sp", tk)

        def finish():
            with nc.Block() as block:
                P.replay(block, sems)

        def smv(name, parts=128):
            o_, n_ = SMO[name]
            return ar[0:parts, o_:o_ + n_]

        ident_bf = smv("ident").bitcast(BF16)
        ones_bf = smv("ones").bitcast(BF16)
        o_cw = SMO["cw"][0]
        cw = smv("cw")
        fw1 = smv("fw1", 33)
        fw2 = smv("fw2", 64)
        fvec = smv("fvec", 64)
        tl = smv("tl")
        hyg = smv("hyg")
        lamin = smv("lamin")
        lamw = smv("lamw")
        subg = smv("subg")
        lnv = smv("lnv")
        lnva = smv("lnva")
        lamtmp = smv("lamtmp")
        epsc = smv("epsc")

        T_SM = T()
        s_sm = dsem("sm")
        P.dma("sp", s_sm, lambda g: [g.dma_start(out=ar[:, 0:SM_USED], in_=smalls_d)], writes=[T_SM])
        if os.environ.get("SM_ONLY") == "1":
            dump_and_finish([(0, 3000)])
            finish()
            return nc
        P.op("dve", TT(fvec[:, 3:4], fvec[:, 0:1], fvec[:, 1:2], ALU.mult), reads=[T_SM], writes=[T_SM])
        P.op("dve", TT(fvec[:, 4:5], fvec[:, 2:3], fvec[:, 1:2], ALU.mult), reads=[T_SM], writes=[T_SM])
        P.op("dve", TS(hyg, hyg, 32.0, None, ALU.mult), reads=[T_SM], writes=[T_SM])
        P.op("dve", TS(subg[:, 0:1], subg[:, 0:1], math.sqrt(128.0) * (1.0 - LAM_INIT), None, ALU.mult),
             reads=[T_SM], writes=[T_SM])
        P.op("dve", TS(lnva, lnv, ALPHA, None, ALU.mult), reads=[T_SM], writes=[T_SM])
        P.op("dve", TT(lamtmp, lamin[:, 0:64], lamin[:, 64:128], ALU.mult), reads=[T_SM], writes=[T_SM])
        P.op("dve", lambda e: e.reduce_sum(out=lamw[:, 0:1], in_=lamtmp, axis=mybir.AxisListType.X),
             reads=[T_SM], writes=[T_SM])
        P.op("dve", TT(lamtmp, lamin[:, 128:192], lamin[:, 192:256], ALU.mult), reads=[T_SM], writes=[T_SM])
        P.op("dve", lambda e: e.reduce_sum(out=lamw[:, 1:2], in_=lamtmp, axis=mybir.AxisListType.X),
             reads=[T_SM], writes=[T_SM])
        P.op("act", ACTF(lamw[:, 2:4], lamw[:, 0:2], AF.Exp), reads=[T_SM], writes=[T_SM])
        P.op("dve", STT(lamw[:, 4:5], lamw[:, 3:4], -LAM_INIT, lamw[:, 2:3], ALU.add, ALU.subtract),
             reads=[T_SM], writes=[T_SM])
        neglam = lamw[:, 4:5]
        if os.environ.get("SM_ONLY") == "2":
            dump_and_finish([(0, 3000)])
            finish()
            return nc

        T_R1 = [T() for _ in range(18)]
        T_S0, T_S1 = T_R1[0], T_R1[1]
        T_P = T_R1[2:]
        xT_bf = bfv(R1, 16384).rearrange("p (a b) -> p a b", a=16)
        x1_bf = bfv(R2, 8192).rearrange("p (a b) -> p a b", a=8)
        T_X1 = [[T() for _ in range(8)] for _ in range(8)]
        vgT = bfv(R3, 8192).rearrange("p (a b) -> p a b", a=16)
        T_VG = [T() for _ in range(8)]
        att = bfv(R3, 8192).rearrange("p (a b) -> p a b", a=8)
        T_ATT = [[T() for _ in range(4)] for _ in range(8)]

        s_x = dsem("xld")

        def load_xT():
            P.dma("pool", s_x,
                  lambda g: [g.dma_start(out=xT_bf[:, dc, :], in_=xT_d[dc]) for dc in range(16)],
                  writes=T_R1, n=16)

        load_xT()
        if stop_after == "A":
            dump_and_finish([(R1, 16384)])
            finish()
            return nc
        o = R4
        wh_bf = [bfv(o, 3072).rearrange("p (a b) -> p a b", a=16), bfv(o + 3072, 3072).rearrange("p (a b) -> p a b", a=16)]
        o += 6144
        T_WH = [T(), T()]
        s_wh = [dsem("wh0"), dsem("wh1")]
        ubuf = [f32v(o, 2050), f32v(o + 2052, 2050)]
        o += 4104
        T_U = [[T() for _ in range(4)] for _ in range(2)]
        zbuf = [f32v(o, 2048), f32v(o + 2048, 2048)]
        o += 4096
        T_Z = [T(), T()]
        vg_bf = bfv(o, 1024)
        o += 1024
        T_VGB = T()
        assert o - R4 <= R4_WORDS
        T_PAD = T()
        for ui in range(2):
            P.op("dve", lambda e, ui=ui: e.memset(ubuf[ui][:, 0:1], 0.0), writes=[T_PAD])
            P.op("dve", lambda e, ui=ui: e.memset(ubuf[ui][:, 2049:2050], 0.0), writes=[T_PAD])
        BANKS_B = [0, 1, 2, 3, 4, 5]
        BANKS_T = [6, 7]
        ucount = 0
        for cb in range(8):
            sl = cb % 2
            P.dma("pool", s_wh[sl], lambda g, cb=cb, sl=sl: [g.dma_start(out=wh_bf[sl], in_=wh_d[cb].rearrange("p (a b) -> p a b", a=16))],
                  writes=[T_WH[sl]])
            for j in (1, 2, 0):
                ui = ucount % 2
                ucount += 1
                ub = ubuf[ui]
                for tg in range(4):
                    bk = nextbank(BANKS_B)
                    for dc in range(16):
                        P.op("pe", MM(bank(bk), wh_bf[sl][:, dc, j * 128:(j + 1) * 128], xT_bf[:, dc, tg * 512:(tg + 1) * 512],
                                      dc == 0, dc == 15),
                             reads=[T_WH[sl]] + T_R1, writes=[TB[bk]], signal=(dc == 15))
                    P.op("act", lambda e, bk=bk, ub=ub, tg=tg: e.copy(out=ub[:, 1 + tg * 512:1 + (tg + 1) * 512], in_=bank(bk)),
                         reads=[TB[bk]], writes=[T_U[ui][tg]])
                cbase = o_cw + cb * 12 + j * 4
                cwv = lambda k, cbase=cbase: ar[:, cbase + k:cbase + k + 1]
                zi = 0 if j in (1, 0) else 1
                z = zbuf[zi]
                P.op("dve", TS(z, ub[:, 0:2048], cwv(0), cwv(3), ALU.mult, ALU.add),
                     reads=T_U[ui] + [T_PAD, T_SM], writes=[T_Z[zi]])
                P.op("dve", STT(z, ub[:, 1:2049], cwv(1), z, ALU.mult, ALU.add), reads=T_U[ui] + [T_SM], writes=[T_Z[zi]])
                if j == 0:
                    P.op("dve", STT(x1_bf[:, cb, :], ub[:, 2:2050], cwv(2), z, ALU.mult, ALU.add),
                         reads=T_U[ui] + [T_Z[zi], T_PAD], writes=T_X1[cb])
                else:
                    P.op("dve", STT(z, ub[:, 2:2050], cwv(2), z, ALU.mult, ALU.add), reads=T_U[ui] + [T_PAD], writes=[T_Z[zi]])
                if j == 2:
                    P.op("dve", TT(vg_bf, zbuf[0], zbuf[1], ALU.mult), reads=[T_Z[0], T_Z[1]], writes=[T_VGB])
                    for half in range(2):
                        bk = nextbank(BANKS_T)
                        bb = bankbf(bk)
                        for i in range(8):
                            tb = half * 8 + i
                            P.op("pe", lambda e, bb=bb, i=i, tb=tb: e.transpose(out=bb[:, i * 128:(i + 1) * 128],
                                                                                in_=vg_bf[:, tb * 128:(tb + 1) * 128],
                                                                                identity=ident_bf),
                                 reads=[T_VGB, T_SM], writes=[TB[bk]], signal=(i == 7))
                        P.op("act", lambda e, bb=bb, half=half, cb=cb: e.copy(
                            out=vgT[:, half * 8:half * 8 + 8, cb * 128:(cb + 1) * 128],
                            in_=bb.rearrange("p (a b) -> p a b", a=8)),
                             reads=[TB[bk]], writes=[T_VG[cb]])
        if stop_after == "B":
            dump_and_finish([(R2, 8192), (R3, 8192)])
            finish()
            return nc

        P.barrier()
        o = R4
        H2T = f32v(o, 2048, 64)
        o += 2048
        w3sb = f32v(o, 2048, 64)
        o += 2048
        ndsb = f32v(o, 1024)
        o += 1024
        dsk = f32v(o, 1024, 1)
        o += 1024
        ksb = [[f32v(o + (2 * i + k) * 512, 512) for k in range(2)] for i in range(2)]
        o += 2048
        tmpc = [f32v(o + i * 512, 512) for i in range(4)]
        o += 2048
        fslot = [bfv(o + i * 2048, 2048).rearrange("p (a b c) -> p a b c", a=2, b=16) for i in range(2)]
        o_fs = o
        o += 4096
        assert o - R4 <= R4_WORDS, o - R4
        zTsb = f32v(o_fs, 2048, 33)
        H1T = f32v(o_fs + 2048, 2048, 64)
        argt = [tmpc[0][0:64, :], tmpc[1][0:64, :]]
        T_C = T()
        T_FS = [T(), T()]
        T_TMP = [T() for _ in range(4)]
        s_c = dsem("cld")
        P.dma("sp", s_c, lambda g: [g.dma_start(out=ar[:, R4 + 2048:R4 + 6144], in_=csm_d),
                                    g.dma_start(out=ar[:, o_fs:o_fs + 2048], in_=zT_d)], writes=[T_C, T_FS[0]], n=2)
        BK = [0, 1, 2, 3]
        T_H1, T_H2, T_ARG = T_FS[1], T(), [T_TMP[0], T_TMP[1]]
        for layer in range(2):
            for tg in range(4):
                bk = nextbank(BK)
                if layer == 0:
                    P.op("pe", MM(bank(bk)[0:64, :], fw1[0:33, 0:64], zTsb[0:33, tg * 512:(tg + 1) * 512], True, True),
                         reads=[T_C, T_FS[0], T_SM], writes=[TB[bk]])
                    bcol = fvec[:, 3:4]
                    dst, Td = H1T, T_H1
                else:
                    P.op("pe", MM(bank(bk)[0:64, :], fw2[0:64, 0:64], H1T[0:64, tg * 512:(tg + 1) * 512], True, True),
                         reads=[T_H1, T_SM], writes=[TB[bk]])
                    bcol = fvec[:, 4:5]
                    dst, Td = H2T, T_H2
                ai = tg % 2
                P.op("dve", TS(argt[ai], bank(bk)[0:64, :], fvec[:, 1:2], bcol, ALU.mult, ALU.add),
                     reads=[TB[bk], T_SM], writes=[T_ARG[ai]])
                zt = tmpc[2 + ai][0:64, :]
                MAGIC = 12582912.0
                P.op("dve", TS(zt, argt[ai], 1.0 / (2.0 * PI), MAGIC, ALU.mult, ALU.add), reads=[T_ARG[ai]], writes=[T_TMP[2 + ai]])
                P.op("dve", TS(zt, zt, MAGIC, -2.0 * PI, ALU.subtract, ALU.mult), reads=[T_TMP[2 + ai]], writes=[T_TMP[2 + ai]])
                P.op("dve", TT(argt[ai], zt, argt[ai], ALU.add), reads=[T_TMP[2 + ai], T_ARG[ai]], writes=[T_ARG[ai]])
                P.op("act", ACTF(dst[0:64, tg * 512:(tg + 1) * 512], argt[ai], AF.Sin, scale=0.999999),
                     reads=[T_ARG[ai]], writes=[Td])
        hsv = bfv(R1, 4096).rearrange("p (a b) -> p a b", a=16)
        hdv = bfv(R1 + 4096, 4096).rearrange("p (a b) -> p a b", a=16)
        islot = [bfv(R1 + i * 4096, 4096).rearrange("p (a b c) -> p a b c", a=2, b=16) for i in range(2)]
        T_IS = [T_S0, T_S1]
        Pbuf = bfv(R1 + 8192, 8192).rearrange("p (a b c) -> p a b c", a=16, b=2)
        s_fs = [dsem("fs0"), dsem("fs1")]
        s_is = [dsem("is0"), dsem("is1")]
        T_K = [[T(), T()], [T(), T()]]
        fcount = 0
        for g in range(2):
            for lb in range(16):
                bA = nextbank(BK)
                bB = nextbank(BK)
                P.op("pe", MM(bank(bA), H2T[0:64, lb * 128:(lb + 1) * 128], w3sb[0:64, g * 512:(g + 1) * 512], True, True),
                     reads=[T_H2, T_C], writes=[TB[bA]])
                P.op("pe", MM(bank(bB), H2T[0:64, lb * 128:(lb + 1) * 128], w3sb[0:64, 1024 + g * 512:1024 + (g + 1) * 512], True, True),
                     reads=[T_H2, T_C], writes=[TB[bB]])
                P.op("act", ACTF(tmpc[0], ndsb[:, g * 512:(g + 1) * 512], AF.Exp, scale=tl[:, lb:lb + 1]),
                     reads=[T_C, T_SM], writes=[T_TMP[0]])
                P.op("dve", TT(tmpc[1], bank(bA), tmpc[0], ALU.mult), reads=[TB[bA], T_TMP[0]], writes=[T_TMP[1]])
                P.op("dve", TT(tmpc[2], bank(bB), tmpc[0], ALU.mult), reads=[TB[bB], T_TMP[0]], writes=[T_TMP[2]])
                if lb == 0:
                    P.op("dve", lambda e: e.memset(tmpc[2][0:1, :], 0.0), writes=[T_TMP[2]])
                    P.op("dve", TT(tmpc[1][0:1, :], tmpc[1][0:1, :], dsk[0:1, g * 512:(g + 1) * 512], ALU.add),
                         reads=[T_C], writes=[T_TMP[1]])
                P.op("dve", TT(hsv[:, lb, :], tmpc[1], tmpc[2], ALU.add), reads=[T_TMP[1], T_TMP[2]], writes=[T_S0])
                P.op("dve", TT(hdv[:, lb, :], tmpc[1], tmpc[2], ALU.subtract), reads=[T_TMP[1], T_TMP[2]], writes=[T_S1])
            for fb in range(16):
                sl = fcount % 2
                fcount += 1
                P.dma("sp", s_fs[sl], lambda g_, fb=fb, sl=sl: [g_.dma_start(
                    out=fslot[sl], in_=ff_d[fb].rearrange("p (a b c) -> p a b c", a=2, b=16))],
                      writes=[T_FS[sl]])
                fs = fslot[sl]
                st = fb % 2
                bks = [4 * st + i for i in range(4)]
                for tc in range(16):
                    last = tc == 15
                    rv = vgT[:, tc, g * 512:(g + 1) * 512]
                    P.op("pe", MM(bank(bks[0]), fs[:, 0, tc, :], rv, tc == 0, last), reads=[T_FS[sl]] + T_VG[4 * g:4 * g + 4],
                         writes=[TB[bks[0]]], signal=False)
                    P.op("pe", MM(bank(bks[1]), fs[:, 0, tc, :], hsv[:, tc, :], tc == 0, last), reads=[T_FS[sl], T_S0],
                         writes=[TB[bks[1]]], signal=False)
                    P.op("pe", MM(bank(bks[2]), fs[:, 1, tc, :], rv, tc == 0, last), reads=[T_FS[sl]],
                         writes=[TB[bks[2]]], signal=False)
                    P.op("pe", MM(bank(bks[3]), fs[:, 1, tc, :], hdv[:, tc, :], tc == 0, last), reads=[T_FS[sl], T_S1],
                         writes=[TB[bks[3]]], signal=last)
                kre, kim = ksb[st]
                P.op("act", lambda e, kre=kre, b=bks[1]: e.copy(out=kre, in_=bank(b)), reads=[TB[bks[1]]], writes=[T_K[st][0]])
                P.op("act", lambda e, kim=kim, b=bks[3]: e.copy(out=kim, in_=bank(b)), reads=[TB[bks[3]]], writes=[T_K[st][1]])
                P.op("dve", TT(tmpc[0], bank(bks[0]), kre, ALU.mult), reads=[TB[bks[0]], T_K[st][0]], writes=[T_TMP[0]])
                P.op("dve", TT(tmpc[1], bank(bks[2]), kim, ALU.mult), reads=[TB[bks[2]], T_K[st][1]], writes=[T_TMP[1]])
                P.op("dve", TT(Pbuf[:, fb, 0, :], tmpc[0], tmpc[1], ALU.subtract), reads=[T_TMP[0], T_TMP[1]], writes=[T_P[fb]])
                P.op("dve", TT(tmpc[2], bank(bks[0]), kim, ALU.mult), reads=[TB[bks[0]], T_K[st][1]], writes=[T_TMP[2]])
                P.op("dve", TT(tmpc[3], bank(bks[2]), kre, ALU.mult), reads=[TB[bks[2]], T_K[st][0]], writes=[T_TMP[3]])
                P.op("dve", TT(Pbuf[:, fb, 1, :], tmpc[2], tmpc[3], ALU.add), reads=[T_TMP[2], T_TMP[3]], writes=[T_P[fb]])
            for ts in range(8):
                sl = ts % 2
                P.dma("sp", s_is[sl], lambda g_, ts=ts, sl=sl: [g_.dma_start(
                    out=islot[sl], in_=fi_d[ts].rearrange("p (a b c) -> p a b c", a=2, b=16))],
                      writes=[T_IS[sl]])
                isl = islot[sl]
                for cbl in range(4):
                    cb = 4 * g + cbl
                    bk = nextbank([0, 1, 2, 3, 4, 5, 6, 7])
                    for fc in range(16):
                        for part in range(2):
                            first = (fc == 0 and part == 0)
                            last = (fc == 15 and part == 1)
                            P.op("pe", MM(bank(bk, 256), Pbuf[:, fc, part, cbl * 128:(cbl + 1) * 128], isl[:, part, fc, :], first, last),
                                 reads=[T_P[fc], T_IS[sl]], writes=[TB[bk]], signal=last)
                    xs = x1_bf[:, cb, ts * 256:(ts + 1) * 256]
                    P.op("dve", TT(xs, bank(bk, 256), xs, ALU.mult), reads=[TB[bk], T_X1[cb][ts]], writes=[T_X1[cb][ts]])
        sqb = [ksb[0][i].bitcast(BF16)[:, 0:512] for i in range(2)]
        T_SQ = [T_K[0][0], T_K[0][1]]
        rstd = tmpc[0]
        for tg in range(4):
            bk = nextbank(BK)
            for cb in range(8):
                si = cb % 2
                P.op("act", ACTF(sqb[si], x1_bf[:, cb, tg * 512:(tg + 1) * 512], AF.Square),
                     reads=T_X1[cb][2 * tg:2 * tg + 2], writes=[T_SQ[si]])
                P.op("pe", MM(bank(bk), ones_bf, sqb[si], cb == 0, cb == 7), reads=[T_SQ[si], T_SM], writes=[TB[bk]],
                     signal=True)
            P.op("act", ACTF(rstd, bank(bk), AF.Sqrt, bias=epsc[:, 0:1]), reads=[TB[bk], T_SM], writes=[T_TMP[0]])
            P.op("dve", lambda e: e.reciprocal(out=rstd, in_=rstd), reads=[T_TMP[0]], writes=[T_TMP[0]])
            for cb in range(8):
                xs = x1_bf[:, cb, tg * 512:(tg + 1) * 512]
                P.op("dve", STT(xs, xs, hyg[:, cb:cb + 1], rstd, ALU.mult, ALU.mult),
                     reads=[T_TMP[0], T_SM] + T_X1[cb][2 * tg:2 * tg + 2], writes=T_X1[cb][2 * tg:2 * tg + 2])
        if stop_after == "C":
            dump_and_finish([(R2, 8192)])
            finish()
            return nc

        P.barrier()
        load_xT()
        o = R4
        wq_bf = bfv(o, 3072).rearrange("p (a b) -> p a b", a=16)
        o += 3072
        T_WQ = T()
        s_wq = dsem("wq")
        qT = bfv(o, 1024)
        kT = bfv(o + 1024, 1024)
        vh = bfv(o + 2048, 1024).rearrange("p (a b) -> p a b", a=16)
        o += 3072
        T_Q, T_KK, T_V = [T() for _ in range(4)], [T() for _ in range(4)], [T() for _ in range(4)]
        Wt = bfv(o, 1984)
        o += 1984
        T_WT = T()
        distb = f32v(o, 3968)
        o += 3968
        T_DIST = T()
        NE = 8
        ering = [bfv(o + i * 256, 256) for i in range(NE)]
        o += NE * 256
        T_E = [T() for _ in range(NE)]
        ntmp = [f32v(o + i * 512, 512) for i in range(4)]
        o += 2048
        T_N = [T() for _ in range(4)]
        assert o - R4 <= R4_WORDS, o - R4
        s_d = dsem("dist")
        P.dma("sp", s_d, lambda g: [g.dma_start(out=distb, in_=dist_d)], writes=[T_DIST])
        SB = [0, 1, 2, 3]
        ecount = 0
        for h in range(NH):
            slope = 2.0 ** (-8.0 * (h + 1) / NH)
            P.dma("pool", s_wq, lambda g, h=h: [g.dma_start(out=wq_bf, in_=wqkv_d[h].rearrange("p (a b) -> p a b", a=16))],
                  writes=[T_WQ])
            P.op("act", ACTF(Wt, distb, AF.Exp, scale=-slope), reads=[T_DIST], writes=[T_WT])
            for which, dst, Td in ((0, qT, T_Q), (1, kT, T_KK)):
                for tg in range(4):
                    bk = nextbank(SB)
                    for dc in range(16):
                        P.op("pe", MM(bank(bk), wq_bf[:, dc, which * 128:(which + 1) * 128], xT_bf[:, dc, tg * 512:(tg + 1) * 512],
                                      dc == 0, dc == 15), reads=[T_WQ] + T_R1, writes=[TB[bk]], signal=(dc == 15))
                    P.op("act", lambda e, dst=dst, tg=tg, bk=bk: e.copy(out=dst[:, tg * 512:(tg + 1) * 512], in_=bank(bk)),
                         reads=[TB[bk]], writes=[Td[tg]])
            for t4 in range(4):
                bk = nextbank(SB)
                for i in range(4):
                    tb = t4 * 4 + i
                    for dc in range(16):
                        P.op("pe", MM(bank(bk)[:, i * 128:(i + 1) * 128], xT_bf[:, dc, tb * 128:(tb + 1) * 128], wq_bf[:, dc, 256:384],
                                      dc == 0, dc == 15), reads=[T_WQ] + T_R1, writes=[TB[bk]], signal=(dc == 15 and i == 3))
                P.op("act", lambda e, t4=t4, bk=bk: e.copy(out=vh[:, t4 * 4:t4 * 4 + 4, :],
                                                            in_=bank(bk).rearrange("p (a b) -> p a b", a=4)),
                     reads=[TB[bk]], writes=[T_V[t4]])
            for qg in range(4):
                OB = [4, 5]
                DB = [6, 7]
                for kb in range(16):
                    u0 = 512 * qg - 128 * kb + 1920
                    for c in range(2):
                        bk = nextbank(SB)
                        P.op("pe", MM(bank(bk), kT[64 * c:64 * c + 64, kb * 128:(kb + 1) * 128],
                                      qT[64 * c:64 * c + 64, qg * 512:(qg + 1) * 512], True, True),
                             reads=[T_KK[kb // 4], T_Q[qg]], writes=[TB[bk]])
                        ei = ecount % NE
                        ecount += 1
                        P.op("act", ACTF(ering[ei], bank(bk), AF.Exp, scale=SCALE), reads=[TB[bk]], writes=[T_E[ei]])
                        P.op("dve", TT(ering[ei], ering[ei], Wt[:, u0:u0 + 512], ALU.mult), reads=[T_WT, T_E[ei]], writes=[T_E[ei]])
                        P.op("pe", MM(bank(OB[c]), vh[:, kb, :], ering[ei], kb == 0, kb == 15),
                             reads=[T_V[kb // 4], T_E[ei]], writes=[TB[OB[c]]], signal=(kb == 15))
                        P.op("pe", MM(bank(DB[c]), ones_bf, ering[ei], kb == 0, kb == 15),
                             reads=[T_E[ei], T_SM], writes=[TB[DB[c]]], signal=True)
                P.op("dve", lambda e: e.reciprocal(out=ntmp[0], in_=bank(6)), reads=[TB[6]], writes=[T_N[0]])
                P.op("dve", lambda e: e.reciprocal(out=ntmp[1], in_=bank(7)), reads=[TB[7]], writes=[T_N[1]])
                P.op("dve", TT(ntmp[0], bank(4), ntmp[0], ALU.mult), reads=[TB[4], T_N[0]], writes=[T_N[0]])
                P.op("dve", TT(ntmp[1], bank(5), ntmp[1], ALU.mult), reads=[TB[5], T_N[1]], writes=[T_N[1]])
                P.op("dve", STT(ntmp[2], ntmp[1], neglam, ntmp[0], ALU.mult, ALU.add),
                     reads=[T_N[0], T_N[1], T_SM], writes=[T_N[2]])
                ei = ecount % NE
                ecount += 1
                P.op("act", ACTF(ering[ei], ntmp[2], AF.Square), reads=[T_N[2]], writes=[T_E[ei]])
                bk = nextbank(SB)
                P.op("pe", MM(bank(bk), ones_bf, ering[ei], True, True), reads=[T_E[ei], T_SM], writes=[TB[bk]])
                P.op("act", ACTF(ntmp[3], bank(bk), AF.Sqrt, bias=epsc[:, 1:2]), reads=[TB[bk], T_SM], writes=[T_N[3]])
                P.op("dve", lambda e: e.reciprocal(out=ntmp[3], in_=ntmp[3]), reads=[T_N[3]], writes=[T_N[3]])
                P.op("dve", STT(att[:, h, qg * 512:(qg + 1) * 512], ntmp[2], subg[:, 0:1], ntmp[3], ALU.mult, ALU.mult),
                     reads=[T_N[2], T_N[3], T_SM], writes=[T_ATT[h][qg]])
        if stop_after == "D":
            dump_and_finish([(R3, 8192)])
            finish()
            return nc

        P.barrier()
        acc = f32v(R1, 16384).rearrange("p (a b) -> p a b", a=16)
        T_ACC = [[T() for _ in range(2)] for _ in range(16)]
        def cat(mc):
            return att[:, mc, :] if mc < 8 else x1_bf[:, mc - 8, :]
        T_CAT = [[T() for _ in range(4)] for _ in range(16)]
        o = R4
        hT = bfv(o, 2048).rearrange("p (a b) -> p a b", a=4)
        o += 2048
        T_HT = [[T() for _ in range(2)] for _ in range(4)]
        wA = bfv(o, 4096)
        wB = bfv(o + 4096, 4096)
        o += 8192
        T_WA, T_WB = T(), T()
        s_wa, s_wb = dsem("wa"), dsem("wb")
        xr = [f32v(o + i * 512, 512) for i in range(2)]
        o += 1024
        T_XR = [T() for _ in range(2)]
        s_xr = [dsem("xr%d" % i) for i in range(2)]
        sbf = [bfv(o + i * 256, 256) for i in range(4)]
        o += 1024
        T_SB = [T() for _ in range(4)]
        lt = [f32v(o + i * 512, 512) for i in range(5)]
        o += 2560
        T_LT = [T() for _ in range(5)]
        ost = [f32v(o + i * 512, 512) for i in range(2)]
        o += 1024
        T_OST = [T(), T()]
        s_ost = [dsem("ost0"), dsem("ost1")]
        assert o - R4 <= R4_WORDS, o - R4
        wo_v = [wA.rearrange("p (a b) -> p a b", a=16), wB.rearrange("p (a b) -> p a b", a=16)]
        w1_v = wA.rearrange("p (a b) -> p a b", a=16)
        w2_v = wB.rearrange("p (a b) -> p a b", a=4)
        T_W = [T_WA, T_WB]
        s_w = [s_wa, s_wb]
        EB = [0, 1, 2, 3, 4, 5]
        xcount = 0
        scount = 0
        ocount = 0

        def ln_stats(hf, tgl, src_reads_fn):
            nonlocal scount
            for db in range(16):
                a_t = acc[:, db, tgl * 512:(tgl + 1) * 512]
                s0 = scount % 4
                s1 = (scount + 1) % 4
                scount += 2
                P.op("act", lambda e, a_t=a_t, s0=s0: e.copy(out=sbf[s0], in_=a_t), reads=[T_ACC[db][tgl]], writes=[T_SB[s0]])
                P.op("act", ACTF(sbf[s1], a_t, AF.Square), reads=[T_ACC[db][tgl]], writes=[T_SB[s1]])
                P.op("pe", MM(bank(6), ones_bf, sbf[s0], db == 0, db == 15), reads=[T_SB[s0], T_SM], writes=[TB[6]], signal=True)
                P.op("pe", MM(bank(7), ones_bf, sbf[s1], db == 0, db == 15), reads=[T_SB[s1], T_SM], writes=[TB[7]], signal=True)
            P.op("dve", TS(lt[0], bank(6), 1.0 / D, None, ALU.mult), reads=[TB[6]], writes=[T_LT[0]])
            P.op("dve", TT(lt[2], lt[0], lt[0], ALU.mult), reads=[T_LT[0]], writes=[T_LT[2]])
            P.op("dve", STT(lt[1], bank(7), 1.0 / D, lt[2], ALU.mult, ALU.subtract), reads=[TB[7], T_LT[2]], writes=[T_LT[1]])
            P.op("act", ACTF(lt[1], lt[1], AF.Sqrt, bias=epsc[:, 2:3]), reads=[T_LT[1], T_SM], writes=[T_LT[1]])
            P.op("dve", lambda e: e.reciprocal(out=lt[1], in_=lt[1]), reads=[T_LT[1]], writes=[T_LT[1]])

        for hf in range(2):
            for tgl in range(2):
                tg = 2 * hf + tgl
                for db in range(16):
                    dbg_, dbl = db // 4, db % 4
                    wsl = dbg_ % 2
                    if dbl == 0:
                        P.dma("pool", s_w[wsl], lambda g, dbg_=dbg_, wsl=wsl: [g.dma_start(
                            out=wo_v[wsl], in_=wo_d[dbg_].rearrange("p (a b) -> p a b", a=16))], writes=[T_W[wsl]])
                    xi = xcount % 2
                    xcount += 1
                    P.dma("sp", s_xr[xi], lambda g, db=db, tg=tg, xi=xi: [g.dma_start(out=xr[xi], in_=xT_d[db][:, tg * 512:(tg + 1) * 512])],
                          writes=[T_XR[xi]])
                    bk = nextbank(EB)
                    for mc in range(16):
                        P.op("pe", MM(bank(bk), wo_v[wsl][:, mc, dbl * 128:(dbl + 1) * 128], cat(mc)[:, tg * 512:(tg + 1) * 512],
                                      mc == 0, mc == 15), reads=[T_W[wsl], T_CAT[mc][tg]], writes=[TB[bk]], signal=(mc == 15))
                    P.op("dve", STT(acc[:, db, tgl * 512:(tgl + 1) * 512], xr[xi], ALPHA, bank(bk), ALU.mult, ALU.add),
                         reads=[T_XR[xi], TB[bk]], writes=[T_ACC[db][tgl]])
                ln_stats(hf, tgl, None)
                for db in range(16):
                    a_t = acc[:, db, tgl * 512:(tgl + 1) * 512]
                    P.op("dve", TT(lt[2], a_t, lt[0], ALU.subtract), reads=[T_ACC[db][tgl], T_LT[0]], writes=[T_LT[2]])
                    P.op("dve", TT(lt[3], lt[2], lt[1], ALU.mult), reads=[T_LT[2], T_LT[1]], writes=[T_LT[3]])
                    P.op("act", ACTF(cat(db)[:, tg * 512:(tg + 1) * 512], lt[3], AF.Identity,
                                     scale=lnv[:, db:db + 1], bias=lnv[:, 16 + db:17 + db]),
                         reads=[T_LT[3], T_SM] + [T_CAT[m][tg] for m in range(16)], writes=[T_CAT[db][tg]])
                    P.op("act", ACTF(a_t, lt[3], AF.Identity, scale=lnva[:, db:db + 1], bias=lnva[:, 16 + db:17 + db]),
                         reads=[T_LT[3], T_SM], writes=[T_ACC[db][tgl]])
            for ft in range(16):
                P.dma("pool", s_wa, lambda g, ft=ft: [g.dma_start(out=w1_v, in_=w1_d[ft].rearrange("p (a b) -> p a b", a=16))],
                      writes=[T_WA])
                P.dma("pool", s_wb, lambda g, ft=ft: [g.dma_start(out=w2_v, in_=w2_d[ft].rearrange("p (a b) -> p a b", a=4))],
                      writes=[T_WB])
                for fbl in range(4):
                    for tgl in range(2):
                        tg = 2 * hf + tgl
                        bk = nextbank(EB)
                        for dc in range(16):
                            P.op("pe", MM(bank(bk), w1_v[:, dc, fbl * 128:(fbl + 1) * 128], cat(dc)[:, tg * 512:(tg + 1) * 512],
                                          dc == 0, dc == 15), reads=[T_WA, T_CAT[dc][tg]], writes=[TB[bk]], signal=(dc == 15))
                        P.op("act", ACTF(lt[4], bank(bk), AF.Relu), reads=[TB[bk]], writes=[T_LT[4]])
                        P.op("act", ACTF(hT[:, fbl, tgl * 512:(tgl + 1) * 512], lt[4], AF.Square), reads=[T_LT[4]],
                             writes=[T_HT[fbl][tgl]])
                for db in range(16):
                    for tgl in range(2):
                        bk = nextbank(EB)
                        for fc in range(4):
                            P.op("pe", MM(bank(bk), w2_v[:, fc, db * 128:(db + 1) * 128], hT[:, fc, tgl * 512:(tgl + 1) * 512],
                                          fc == 0, fc == 3), reads=[T_WB, T_HT[fc][tgl]], writes=[TB[bk]], signal=(fc == 3))
                        a_t = acc[:, db, tgl * 512:(tgl + 1) * 512]
                        P.op("dve", TT(a_t, bank(bk), a_t, ALU.add), reads=[TB[bk], T_ACC[db][tgl]], writes=[T_ACC[db][tgl]])
            for tgl in range(2):
                tg = 2 * hf + tgl
                ln_stats(hf, tgl, None)
                for db in range(16):
                    a_t = acc[:, db, tgl * 512:(tgl + 1) * 512]
                    P.op("dve", TT(lt[2], a_t, lt[0], ALU.subtract), reads=[T_ACC[db][tgl], T_LT[0]], writes=[T_LT[2]])
                    P.op("dve", TT(lt[3], lt[2], lt[1], ALU.mult), reads=[T_LT[2], T_LT[1]], writes=[T_LT[3]])
                    oi = ocount % 2
                    ocount += 1
                    P.op("act", ACTF(ost[oi], lt[3], AF.Identity, scale=lnv[:, 32 + db:33 + db], bias=lnv[:, 48 + db:49 + db]),
                         reads=[T_LT[3], T_SM], writes=[T_OST[oi]])
                    P.dma("sp", s_ost[oi], lambda g, oi=oi, db=db, tg=tg: [g.dma_start(out=yT_d[db][:, tg * 512:(tg + 1) * 512], in_=ost[oi])],
                          reads=[T_OST[oi]])
        P.barrier()
        for s_ in s_ost:
            P._wait("sp", (s_, P.dma_cnt[s_]))
        finish()
    return nc


_CONST = {}


def _consts():
    if _CONST:
        return _CONST
    bf = ml_dtypes.bfloat16
    f = np.arange(2048, dtype=np.float64)
    t = np.arange(2048, dtype=np.float64)
    theta = 2.0 * np.pi * (f + 0.5) / NFFT
    ang = np.outer(t, theta)
    cf = np.cos(ang)
    sf = -np.sin(ang)
    F = np.stack([cf, sf], 0).reshape(2, 16, 128, 16, 128)
    dft_f = np.ascontiguousarray(F.transpose(3, 2, 0, 1, 4)).reshape(16, 128, 2 * 16 * 128).astype(bf)
    ci = (2.0 / NFFT) * cf.T
    si = (2.0 / NFFT) * sf.T
    G = np.stack([ci, si], 0).reshape(2, 16, 128, 8, 256)
    dft_i = np.ascontiguousarray(G.transpose(3, 2, 0, 1, 4)).reshape(8, 128, 2 * 16 * 256).astype(bf)
    tt = np.linspace(0.0, 1.0, L, dtype=np.float32)[:, None]
    bands = 16
    w = (2.0 * np.pi * np.arange(L, dtype=np.float32)[:, None] / L).astype(np.float32)
    fr = np.linspace(1e-4, bands - 1, bands, dtype=np.float32)[None, :]
    z = np.concatenate([tt, np.cos(fr * w), -np.sin(fr * w)], axis=-1).astype(np.float32)
    zT = np.ascontiguousarray(z.T)
    tl = np.ascontiguousarray(tt[:, 0].reshape(16, 128).T)
    max_decay = math.log(1e-2) / 0.3
    min_decay = math.log(1e-2) / 1.5
    deltas = np.linspace(min_decay, max_decay, C, dtype=np.float32)
    negdelta = np.ascontiguousarray(np.broadcast_to(-np.abs(deltas)[None, :], (128, C))).astype(np.float32)
    u = np.arange(3968, dtype=np.float32)[None, :]
    p = np.arange(128, dtype=np.float32)[:, None]
    dist = np.abs(u - 1920.0 - p).astype(np.float32)
    ident = np.eye(128, dtype=np.float32).astype(bf)
    _CONST.update(dft_f=dft_f, dft_i=dft_i, zT=zT, tl=tl, negdelta=negdelta, dist=dist, ident=ident)
    return _CONST


def _prep_weights(inp):
    f32 = np.float32
    w_in = np.asarray(inp["w_in"], f32)[0]
    wr = w_in.reshape(16, 128, 6144)
    hcols = wr[:, :, 3 * A:].reshape(16, 128, 3, 8, 128)
    wh = np.ascontiguousarray(hcols.transpose(3, 1, 0, 2, 4)).reshape(8, 128, 16 * 384)
    q = wr[:, :, 0:A].reshape(16, 128, 8, 128)
    k = wr[:, :, A:2 * A].reshape(16, 128, 8, 128)
    v = wr[:, :, 2 * A:3 * A].reshape(16, 128, 8, 128)
    qkv = np.stack([q, k, v], 3)
    wqkv = np.ascontiguousarray(qkv.transpose(2, 1, 0, 3, 4)).reshape(8, 128, 16 * 384)
    w_out = np.asarray(inp["w_out"], f32)[0].reshape(16, 128, 4, 512)
    wo = np.ascontiguousarray(w_out.transpose(2, 1, 0, 3)).reshape(4, 128, 16 * 512)
    w_ff1 = np.asarray(inp["w_ff1"], f32)[0].reshape(16, 128, 16, 512)
    w1 = np.ascontiguousarray(w_ff1.transpose(2, 1, 0, 3)).reshape(16, 128, 16 * 512)
    w_ff2 = np.asarray(inp["w_ff2"], f32)[0].reshape(16, 4, 128, 2048)
    w2 = np.ascontiguousarray(w_ff2.transpose(0, 2, 1, 3)).reshape(16, 128, 4 * 2048)
    conv_w = np.asarray(inp["conv_w"], f32)[0]
    conv_b = np.asarray(inp["conv_b"], f32)[0]
    cwb = np.concatenate([conv_w, conv_b[None, :]], 0).reshape(4, 3, 8, 128)
    cw = np.ascontiguousarray(cwb.transpose(3, 2, 1, 0)).reshape(128, 96)
    fvec = np.stack([np.asarray(inp["filt_b1"], f32)[0], np.asarray(inp["filt_freq"], f32)[0],
                     np.asarray(inp["filt_b2"], f32)[0]], 1)
    lam = np.concatenate([np.asarray(inp[n], f32)[0] for n in ("lambda_q1", "lambda_k1", "lambda_q2", "lambda_k2")])
    lamin = np.ascontiguousarray(np.broadcast_to(lam[None, :], (128, 256)))
    lnv = np.concatenate([np.asarray(inp[n], f32)[0].reshape(16, 128).T for n in ("ln1_g", "ln1_b", "ln2_g", "ln2_b")], 1)
    bf = ml_dtypes.bfloat16
    sm = np.zeros((128, SM_USED), np.float32)

    def put(name, arr):
        o_, n_ = SMO[name]
        arr = np.asarray(arr, np.float32)
        sm[:arr.shape[0], o_:o_ + arr.shape[1]] = arr

    put("ident", np.eye(128, dtype=np.float32).astype(bf).view(np.float32))
    put("ones", np.ones((128, 128), np.float32).astype(bf).view(np.float32))
    put("cw", cw)
    put("fw1", np.asarray(inp["filt_w1"], f32)[0])
    put("fw2", np.asarray(inp["filt_w2"], f32)[0])
    put("fvec", fvec)
    put("tl", _consts()["tl"])
    put("hyg", np.asarray(inp["hyena_gain"], f32)[0].reshape(8, 128).T)
    put("lamin", lamin)
    put("subg", np.asarray(inp["subln_g"], f32)[0].reshape(128, 1))
    put("lnv", lnv)
    put("epsc", np.broadcast_to(np.array([[1024.0 * EPS, 128.0 * EPS, EPS, 0.0]], np.float32), (128, 4)))
    csm = np.zeros((128, 4096), np.float32)
    csm[:64, 0:2048] = np.asarray(inp["filt_w3"], f32)[0]
    csm[:, 2048:3072] = _consts()["negdelta"]
    csm[0, 3072:4096] = np.asarray(inp["hyena_skip"], f32)[0]
    out = dict(wh=wh, wqkv=wqkv, wo=wo, w1=w1, w2=w2, smalls=sm, csm=csm)
    return out


def make_in_maps(inp):
    c = _consts()
    shared = _prep_weights(inp)
    zTp = np.zeros((128, 2048), np.float32)
    zTp[:33] = c["zT"]
    shared.update(dft_f=c["dft_f"], dft_i=c["dft_i"], zTp=zTp, dist=c["dist"])
    x = np.asarray(inp["x"], np.float32)
    maps = []
    for b in range(8):
        xT = np.ascontiguousarray(x[b].T).reshape(16, 128, 2048)
        m = dict(shared)
        m["xT"] = xT
        maps.append(m)
    return maps


def kernel(**inputs):
    nc = build_nc()
    maps = make_in_maps(inputs)
    res = run_bass_kernel_spmd(nc, maps, core_ids=list(range(8)))
    out = np.empty((8, S, D), np.float32)
    for b in range(8):
        yT = np.asarray(res.results[b]["yT"]).reshape(D, S)
        out[b] = yT.T
    return out
```

```python
import math
import os
from contextlib import ExitStack
import numpy as np
import ml_dtypes
import concourse.bass as bass
import concourse.mybir as mybir
from concourse.bass_utils import run_bass_kernel_spmd

F32 = mybir.dt.float32
BF16 = mybir.dt.bfloat16
ALU = mybir.AluOpType
AF = mybir.ActivationFunctionType

ENGS = ("pe", "act", "dve", "pool", "sp")

D = 2048
S = 2048
A = 1024
C = 1024
NH = 8
DFF = 8192
L = S
NFFT = 2 * L
EPS = 1e-5
ALPHA = 2.0 ** 0.25
LAM_INIT = 0.8 - 0.6 * math.exp(0.0)
SCALE = 64 ** -0.5
PI = math.pi


class T:
    __slots__ = ("w", "r")

    def __init__(self):
        self.w = None
        self.r = {}


class Prog:
    def __init__(self):
        self.ops = {e: [] for e in ENGS}
        self.cnt = {e: 0 for e in ENGS}
        self.waited = {e: {} for e in ENGS}
        self.dma_cnt = {}

    def _wait(self, eng, tok):
        if tok is None:
            return
        key, val = tok
        if eng == "pe" and key == "pe":
            return
        if val > self.waited[eng].get(key, 0):
            self.waited[eng][key] = val
            self.ops[eng].append(("wait", key, val))

    def new_dma_sem(self, name):
        self.dma_cnt[name] = 0
        return name

    def _deps(self, eng, reads, writes):
        for t in reads:
            self._wait(eng, t.w)
        for t in writes:
            self._wait(eng, t.w)
            for k, v in t.r.items():
                self._wait(eng, (k, v))

    def _mark(self, tok, reads, writes):
        for t in reads:
            if t.r.get(tok[0], 0) < tok[1]:
                t.r[tok[0]] = tok[1]
        for t in writes:
            t.w = tok
            t.r = {}

    def op(self, eng, fn, reads=(), writes=(), signal=True):
        self._deps(eng, reads, writes)
        if signal:
            self.cnt[eng] += 1
            tok = (eng, self.cnt[eng])
        else:
            tok = (eng, self.cnt[eng] + 1)
        self.ops[eng].append(("op", fn, signal))
        self._mark(tok, reads, writes)
        return tok

    def dma(self, eng, sem, fn, reads=(), writes=(), n=1):
        self._deps(eng, reads, writes)
        self.dma_cnt[sem] += 16 * n
        tok = (sem, self.dma_cnt[sem])
        self.ops[eng].append(("dma", fn, sem))
        self._mark(tok, reads, writes)
        return tok

    def barrier(self):
        for e in ENGS:
            for f in ENGS:
                if f != e and self.cnt[f] > 0:
                    self._wait(e, (f, self.cnt[f]))
            for s, v in self.dma_cnt.items():
                if v > 0:
                    self._wait(e, (s, v))

    def replay(self, block, sems):
        engmap = {"pe": "tensor", "act": "scalar", "dve": "vector", "pool": "gpsimd", "sp": "sync"}
        prog = self

        def make(e):
            def body(engine):
                for item in prog.ops[e]:
                    if item[0] == "wait":
                        engine.wait_ge(sems[item[1]], item[2])
                    elif item[0] == "op":
                        ins = item[1](engine)
                        if item[2]:
                            ins.then_inc(sems[e], 1)
                    else:
                        res = item[1](engine)
                        if not isinstance(res, (list, tuple)):
                            res = [res]
                        for ins in res:
                            ins.then_inc(sems[item[2]], 16)
            return body

        for e in ENGS:
            getattr(block, engmap[e])(make(e))


def MM(out, lhsT, rhs, start, stop):
    return lambda e: e.matmul(out, lhsT, rhs, start=start, stop=stop)


def ACTF(out, in_, func, **kw):
    return lambda e: e.activation(out=out, in_=in_, func=func, **kw)


def TT(out, in0, in1, op):
    return lambda e: e.tensor_tensor(out=out, in0=in0, in1=in1, op=op)


def TS(out, in0, s1, s2, op0, op1=None):
    if op1 is None:
        return lambda e: e.tensor_scalar(out=out, in0=in0, scalar1=s1, scalar2=None, op0=op0)
    return lambda e: e.tensor_scalar(out=out, in0=in0, scalar1=s1, scalar2=s2, op0=op0, op1=op1)


def STT(out, in0, scalar, in1, op0, op1):
    return lambda e: e.scalar_tensor_tensor(out=out, in0=in0, scalar=scalar, in1=in1, op0=op0, op1=op1)


SMO = {}
_o = 0
for _n, _w in (("ident", 64), ("ones", 64), ("cw", 96), ("fw1", 64), ("fw2", 64), ("fvec", 8), ("tl", 16), ("hyg", 8),
               ("lamin", 256), ("lamw", 8), ("subg", 2), ("lnv", 64), ("lnva", 64), ("lamtmp", 64), ("epsc", 4)):
    SMO[_n] = (_o, _w)
    _o += _w
SM_USED = _o
ARENA = 52000
SM_WORDS = 3000
R1 = SM_WORDS
R2 = R1 + 16384
R3 = R2 + 8192
R4 = R3 + 8192
R4_WORDS = ARENA - R4


def build_nc(stop_after=None, dbg_words=0):
    nc = bass.Bass("TRN2", target_bir_lowering=False)
    dr = {}

    order = ["A", "B", "C", "D", None]
    lvl = order.index(stop_after)

    def din(name, shape, dt=F32, need="A"):
        if order.index(need) > lvl:
            return None
        dr[name] = nc.dram_tensor(name, list(shape), dt, kind="ExternalInput").ap()
        return dr[name]

    xT_d = din("xT", [16, 128, 2048])
    wh_d = din("wh", [8, 128, 16 * 384])
    wqkv_d = din("wqkv", [8, 128, 16 * 384], need="D")
    wo_d = din("wo", [4, 128, 16 * 512], need=None)
    w1_d = din("w1", [16, 128, 16 * 512], need=None)
    w2_d = din("w2", [16, 128, 4 * 2048], need=None)
    ff_d = din("dft_f", [16, 128, 2 * 16 * 128], BF16, need="C")
    fi_d = din("dft_i", [8, 128, 2 * 16 * 256], BF16, need="C")
    dist_d = din("dist", [128, 3968], need="D")
    smalls_d = din("smalls", [128, SM_USED])
    csm_d = din("csm", [128, 4096], need="C")
    zT_d = din("zTp", [128, 2048], need="C")
    yT_d = nc.dram_tensor("yT", [16, 128, 2048], F32, kind="ExternalOutput").ap()
    dbg_d = None
    if dbg_words:
        dbg_d = nc.dram_tensor("dbg", [128, dbg_words], F32, kind="ExternalOutput").ap()

    nc._declared_inputs = list(dr.keys())
    P = Prog()
    es = ExitStack()
    with es:
        ar = es.enter_context(nc.sbuf_tensor("arena", [128, ARENA], F32))
        ps = es.enter_context(nc.psum_tensor("ps", [128, 4096], F32))
        sems = {e: es.enter_context(nc.semaphore("s_" + e)) for e in ENGS}

        def dsem(name):
            P.new_dma_sem(name)
            sems[name] = es.enter_context(nc.semaphore("d_" + name))
            return name

        def f32v(off, n, parts=128):
            return ar[0:parts, off:off + n]

        def bfv(off, nwords, parts=128):
            return ar[0:parts, off:off + nwords].bitcast(BF16)

        def bank(i, n=512):
            return ps[:, i * 512:i * 512 + n]

        def bankbf(i):
            return ps[:, i * 512:(i + 1) * 512].bitcast(BF16)

        TB = [T() for _ in range(8)]
        bank_rr = [0]

        def nextbank(pool):
            b = pool[bank_rr[0] % len(pool)]
            bank_rr[0] += 1
            return b

        def dump_and_finish(regions):
            P.barrier()
            s_dbg = dsem("dbg")
            off = [0]
            outs = []
            for (o, n) in regions:
                outs.append((dbg_d[:, off[0]:off[0] + n], ar[:, o:o + n]))
                off[0] += n
            tk = P.dma("sp", s_dbg, lambda g: [g.dma_start(out=a, in_=b) for (a, b) in outs], n=len(outs))
            P._wait("sp", tk)

        def finish():
            with nc.Block() as block:
                P.replay(block, sems)

        def smv(name, parts=128):
            o_, n_ = SMO[name]
            return ar[0:parts, o_:o_ + n_]

        ident_bf = smv("ident").bitcast(BF16)
        ones_bf = smv("ones").bitcast(BF16)
        o_cw = SMO["cw"][0]
        cw = smv("cw")
        fw1 = smv("fw1", 33)
        fw2 = smv("fw2", 64)
        fvec = smv("fvec", 64)
        tl = smv("tl")
        hyg = smv("hyg")
        lamin = smv("lamin")
        lamw = smv("lamw")
        subg = smv("subg")
        lnv = smv("lnv")
        lnva = smv("lnva")
        lamtmp = smv("lamtmp")
        epsc = smv("epsc")

        T_SM = T()
        s_sm = dsem("sm")
        P.dma("sp", s_sm, lambda g: [g.dma_start(out=ar[:, 0:SM_USED], in_=smalls_d)], writes=[T_SM])
        if os.environ.get("SM_ONLY") == "1":
            dump_and_finish([(0, 3000)])
            finish()
            return nc
        P.op("dve", TT(fvec[:, 3:4], fvec[:, 0:1], fvec[:, 1:2], ALU.mult), reads=[T_SM], writes=[T_SM])
        P.op("dve", TT(fvec[:, 4:5], fvec[:, 2:3], fvec[:, 1:2], ALU.mult), reads=[T_SM], writes=[T_SM])
        P.op("dve", TS(hyg, hyg, 32.0, None, ALU.mult), reads=[T_SM], writes=[T_SM])
        P.op("dve", TS(subg[:, 0:1], subg[:, 0:1], math.sqrt(128.0) * (1.0 - LAM_INIT), None, ALU.mult),
             reads=[T_SM], writes=[T_SM])
        P.op("dve", TS(lnva, lnv, ALPHA, None, ALU.mult), reads=[T_SM], writes=[T_SM])
        P.op("dve", TT(lamtmp, lamin[:, 0:64], lamin[:, 64:128], ALU.mult), reads=[T_SM], writes=[T_SM])
        P.op("dve", lambda e: e.reduce_sum(out=lamw[:, 0:1], in_=lamtmp, axis=mybir.AxisListType.X),
             reads=[T_SM], writes=[T_SM])
        P.op("dve", TT(lamtmp, lamin[:, 128:192], lamin[:, 192:256], ALU.mult), reads=[T_SM], writes=[T_SM])
        P.op("dve", lambda e: e.reduce_sum(out=lamw[:, 1:2], in_=lamtmp, axis=mybir.AxisListType.X),
             reads=[T_SM], writes=[T_SM])
        P.op("act", ACTF(lamw[:, 2:4], lamw[:, 0:2], AF.Exp), reads=[T_SM], writes=[T_SM])
        P.op("dve", STT(lamw[:, 4:5], lamw[:, 3:4], -LAM_INIT, lamw[:, 2:3], ALU.add, ALU.subtract),
             reads=[T_SM], writes=[T_SM])
        neglam = lamw[:, 4:5]
        if os.environ.get("SM_ONLY") == "2":
            dump_and_finish([(0, 3000)])
            finish()
            return nc

        T_R1 = [T() for _ in range(18)]
        T_S0, T_S1 = T_R1[0], T_R1[1]
        T_P = T_R1[2:]
        xT_bf = bfv(R1, 16384).rearrange("p (a b) -> p a b", a=16)
        x1_bf = bfv(R2, 8192).rearrange("p (a b) -> p a b", a=8)
        T_X1 = [[T() for _ in range(8)] for _ in range(8)]
        vgT = bfv(R3, 8192).rearrange("p (a b) -> p a b", a=16)
        T_VG = [T() for _ in range(8)]
        att = bfv(R3, 8192).rearrange("p (a b) -> p a b", a=8)
        T_ATT = [[T() for _ in range(4)] for _ in range(8)]

        s_x = dsem("xld")

        def load_xT():
            P.dma("pool", s_x,
                  lambda g: [g.dma_start(out=xT_bf[:, dc, :], in_=xT_d[dc]) for dc in range(16)],
                  writes=T_R1, n=16)

        load_xT()
        if stop_after == "A":
            dump_and_finish([(R1, 16384)])
            finish()
            return nc
        o = R4
        wh_bf = [bfv(o, 3072).rearrange("p (a b) -> p a b", a=16), bfv(o + 3072, 3072).rearrange("p (a b) -> p a b", a=16)]
        o += 6144
        T_WH = [T(), T()]
        s_wh = [dsem("wh0"), dsem("wh1")]
        ubuf = [f32v(o, 2050), f32v(o + 2052, 2050)]
        o += 4104
        T_U = [[T() for _ in range(4)] for _ in range(2)]
        zbuf = [f32v(o, 2048), f32v(o + 2048, 2048)]
        o += 4096
        T_Z = [T(), T()]
        vg_bf = bfv(o, 1024)
        o += 1024
        T_VGB = T()
        assert o - R4 <= R4_WORDS
        T_PAD = T()
        for ui in range(2):
            P.op("dve", lambda e, ui=ui: e.memset(ubuf[ui][:, 0:1], 0.0), writes=[T_PAD])
            P.op("dve", lambda e, ui=ui: e.memset(ubuf[ui][:, 2049:2050], 0.0), writes=[T_PAD])
        BANKS_B = [0, 1, 2, 3, 4, 5]
        BANKS_T = [6, 7]
        ucount = 0
        for cb in range(8):
            sl = cb % 2
            P.dma("pool", s_wh[sl], lambda g, cb=cb, sl=sl: [g.dma_start(out=wh_bf[sl], in_=wh_d[cb].rearrange("p (a b) -> p a b", a=16))],
                  writes=[T_WH[sl]])
            for j in (1, 2, 0):
                ui = ucount % 2
                ucount += 1
                ub = ubuf[ui]
                for tg in range(4):
                    bk = nextbank(BANKS_B)
                    for dc in range(16):
                        P.op("pe", MM(bank(bk), wh_bf[sl][:, dc, j * 128:(j + 1) * 128], xT_bf[:, dc, tg * 512:(tg + 1) * 512],
                                      dc == 0, dc == 15),
                             reads=[T_WH[sl]] + T_R1, writes=[TB[bk]], signal=(dc == 15))
                    P.op("act", lambda e, bk=bk, ub=ub, tg=tg: e.copy(out=ub[:, 1 + tg * 512:1 + (tg + 1) * 512], in_=bank(bk)),
                         reads=[TB[bk]], writes=[T_U[ui][tg]])
                cbase = o_cw + cb * 12 + j * 4
                cwv = lambda k, cbase=cbase: ar[:, cbase + k:cbase + k + 1]
                zi = 0 if j in (1, 0) else 1
                z = zbuf[zi]
                P.op("dve", TS(z, ub[:, 0:2048], cwv(0), cwv(3), ALU.mult, ALU.add),
                     reads=T_U[ui] + [T_PAD, T_SM], writes=[T_Z[zi]])
                P.op("dve", STT(z, ub[:, 1:2049], cwv(1), z, ALU.mult, ALU.add), reads=T_U[ui] + [T_SM], writes=[T_Z[zi]])
                if j == 0:
                    P.op("dve", STT(x1_bf[:, cb, :], ub[:, 2:2050], cwv(2), z, ALU.mult, ALU.add),
                         reads=T_U[ui] + [T_Z[zi], T_PAD], writes=T_X1[cb])
                else:
                    P.op("dve", STT(z, ub[:, 2:2050], cwv(2), z, ALU.mult, ALU.add), reads=T_U[ui] + [T_PAD], writes=[T_Z[zi]])
                if j == 2:
                    P.op("dve", TT(vg_bf, zbuf[0], zbuf[1], ALU.mult), reads=[T_Z[0], T_Z[1]], writes=[T_VGB])
                    for half in range(2):
                        bk = nextbank(BANKS_T)
                        bb = bankbf(bk)
                        for i in range(8):
                            tb = half * 8 + i
                            P.op("pe", lambda e, bb=bb, i=i, tb=tb: e.transpose(out=bb[:, i * 128:(i + 1) * 128],
                                                                                in_=vg_bf[:, tb * 128:(tb + 1) * 128],
                                                                                identity=ident_bf),
                                 reads=[T_VGB, T_SM], writes=[TB[bk]], signal=(i == 7))
                        P.op("act", lambda e, bb=bb, half=half, cb=cb: e.copy(
                            out=vgT[:, half * 8:half * 8 + 8, cb * 128:(cb + 1) * 128],
                            in_=bb.rearrange("p (a b) -> p a b", a=8)),
                             reads=[TB[bk]], writes=[T_VG[cb]])
        if stop_after == "B":
            dump_and_finish([(R2, 8192), (R3, 8192)])
            finish()
            return nc

        P.barrier()
        o = R4
        H2T = f32v(o, 2048, 64)
        o += 2048
        w3sb = f32v(o, 2048, 64)
        o += 2048
        ndsb = f32v(o, 1024)
        o += 1024
        dsk = f32v(o, 1024, 1)
        o += 1024
        ksb = [[f32v(o + (2 * i + k) * 512, 512) for k in range(2)] for i in range(2)]
        o += 2048
        tmpc = [f32v(o + i * 512, 512) for i in range(4)]
        o += 2048
        fslot = [bfv(o + i * 2048, 2048).rearrange("p (a b c) -> p a b c", a=2, b=16) for i in range(2)]
        o_fs = o
        o += 4096
        assert o - R4 <= R4_WORDS, o - R4
        zTsb = f32v(o_fs, 2048, 33)
        H1T = f32v(o_fs + 2048, 2048, 64)
        argt = [tmpc[0][0:64, :], tmpc[1][0:64, :]]
        T_C = T()
        T_FS = [T(), T()]
        T_TMP = [T() for _ in range(4)]
        s_c = dsem("cld")
        P.dma("sp", s_c, lambda g: [g.dma_start(out=ar[:, R4 + 2048:R4 + 6144], in_=csm_d),
                                    g.dma_start(out=ar[:, o_fs:o_fs + 2048], in_=zT_d)], writes=[T_C, T_FS[0]], n=2)
        BK = [0, 1, 2, 3]
        T_H1, T_H2, T_ARG = T_FS[1], T(), [T_TMP[0], T_TMP[1]]
        for layer in range(2):
            for tg in range(4):
                bk = nextbank(BK)
                if layer == 0:
                    P.op("pe", MM(bank(bk)[0:64, :], fw1[0:33, 0:64], zTsb[0:33, tg * 512:(tg + 1) * 512], True, True),
                         reads=[T_C, T_FS[0], T_SM], writes=[TB[bk]])
                    bcol = fvec[:, 3:4]
                    dst, Td = H1T, T_H1
                else:
                    P.op("pe", MM(bank(bk)[0:64, :], fw2[0:64, 0:64], H1T[0:64, tg * 512:(tg + 1) * 512], True, True),
                         reads=[T_H1, T_SM], writes=[TB[bk]])
                    bcol = fvec[:, 4:5]
                    dst, Td = H2T, T_H2
                ai = tg % 2
                P.op("dve", TS(argt[ai], bank(bk)[0:64, :], fvec[:, 1:2], bcol, ALU.mult, ALU.add),
                     reads=[TB[bk], T_SM], writes=[T_ARG[ai]])
                zt = tmpc[2 + ai][0:64, :]
                MAGIC = 12582912.0
                P.op("dve", TS(zt, argt[ai], 1.0 / (2.0 * PI), MAGIC, ALU.mult, ALU.add), reads=[T_ARG[ai]], writes=[T_TMP[2 + ai]])
                P.op("dve", TS(zt, zt, MAGIC, -2.0 * PI, ALU.subtract, ALU.mult), reads=[T_TMP[2 + ai]], writes=[T_TMP[2 + ai]])
                P.op("dve", TT(argt[ai], zt, argt[ai], ALU.add), reads=[T_TMP[2 + ai], T_ARG[ai]], writes=[T_ARG[ai]])
                P.op("act", ACTF(dst[0:64, tg * 512:(tg + 1) * 512], argt[ai], AF.Sin, scale=0.999999),
                     reads=[T_ARG[ai]], writes=[Td])
        hsv = bfv(R1, 4096).rearrange("p (a b) -> p a b", a=16)
        hdv = bfv(R1 + 4096, 4096).rearrange("p (a b) -> p a b", a=16)
        islot = [bfv(R1 + i * 4096, 4096).rearrange("p (a b c) -> p a b c", a=2, b=16) for i in range(2)]
        T_IS = [T_S0, T_S1]
        Pbuf = bfv(R1 + 8192, 8192).rearrange("p (a b c) -> p a b c", a=16, b=2)
        s_fs = [dsem("fs0"), dsem("fs1")]
        s_is = [dsem("is0"), dsem("is1")]
        T_K = [[T(), T()], [T(), T()]]
        fcount = 0
        for g in range(2):
            for lb in range(16):
                bA = nextbank(BK)
                bB = nextbank(BK)
                P.op("pe", MM(bank(bA), H2T[0:64, lb * 128:(lb + 1) * 128], w3sb[0:64, g * 512:(g + 1) * 512], True, True),
                     reads=[T_H2, T_C], writes=[TB[bA]])
                P.op("pe", MM(bank(bB), H2T[0:64, lb * 128:(lb + 1) * 128], w3sb[0:64, 1024 + g * 512:1024 + (g + 1) * 512], True, True),
                     reads=[T_H2, T_C], writes=[TB[bB]])
                P.op("act", ACTF(tmpc[0], ndsb[:, g * 512:(g + 1) * 512], AF.Exp, scale=tl[:, lb:lb + 1]),
                     reads=[T_C, T_SM], writes=[T_TMP[0]])
                P.op("dve", TT(tmpc[1], bank(bA), tmpc[0], ALU.mult), reads=[TB[bA], T_TMP[0]], writes=[T_TMP[1]])
                P.op("dve", TT(tmpc[2], bank(bB), tmpc[0], ALU.mult), reads=[TB[bB], T_TMP[0]], writes=[T_TMP[2]])
                if lb == 0:
                    P.op("dve", lambda e: e.memset(tmpc[2][0:1, :], 0.0), writes=[T_TMP[2]])
                    P.op("dve", TT(tmpc[1][0:1, :], tmpc[1][0:1, :], dsk[0:1, g * 512:(g + 1) * 512], ALU.add),
                         reads=[T_C], writes=[T_TMP[1]])
                P.op("dve", TT(hsv[:, lb, :], tmpc[1], tmpc[2], ALU.add), reads=[T_TMP[1], T_TMP[2]], writes=[T_S0])
                P.op("dve", TT(hdv[:, lb, :], tmpc[1], tmpc[2], ALU.subtract), reads=[T_TMP[1], T_TMP[2]], writes=[T_S1])
            for fb in range(16):
                sl = fcount % 2
                fcount += 1
                P.dma("sp", s_fs[sl], lambda g_, fb=fb, sl=sl: [g_.dma_start(
                    out=fslot[sl], in_=ff_d[fb].rearrange("p (a b c) -> p a b c", a=2, b=16))],
                      writes=[T_FS[sl]])
                fs = fslot[sl]
                st = fb % 2
                bks = [4 * st + i for i in range(4)]
                for tc in range(16):
                    last = tc == 15
                    rv = vgT[:, tc, g * 512:(g + 1) * 512]
                    P.op("pe", MM(bank(bks[0]), fs[:, 0, tc, :], rv, tc == 0, last), reads=[T_FS[sl]] + T_VG[4 * g:4 * g + 4],
                         writes=[TB[bks[0]]], signal=False)
                    P.op("pe", MM(bank(bks[1]), fs[:, 0, tc, :], hsv[:, tc, :], tc == 0, last), reads=[T_FS[sl], T_S0],
                         writes=[TB[bks[1]]], signal=False)
                    P.op("pe", MM(bank(bks[2]), fs[:, 1, tc, :], rv, tc == 0, last), reads=[T_FS[sl]],
                         writes=[TB[bks[2]]], signal=False)
                    P.op("pe", MM(bank(bks[3]), fs[:, 1, tc, :], hdv[:, tc, :], tc == 0, last), reads=[T_FS[sl], T_S1],
                         writes=[TB[bks[3]]], signal=last)
                kre, kim = ksb[st]
                P.op("act", lambda e, kre=kre, b=bks[1]: e.copy(out=kre, in_=bank(b)), reads=[TB[bks[1]]], writes=[T_K[st][0]])
                P.op("act", lambda e, kim=kim, b=bks[3]: e.copy(out=kim, in_=bank(b)), reads=[TB[bks[3]]], writes=[T_K[st][1]])
                P.op("dve", TT(tmpc[0], bank(bks[0]), kre, ALU.mult), reads=[TB[bks[0]], T_K[st][0]], writes=[T_TMP[0]])
                P.op("dve", TT(tmpc[1], bank(bks[2]), kim, ALU.mult), reads=[TB[bks[2]], T_K[st][1]], writes=[T_TMP[1]])
                P.op("dve", TT(Pbuf[:, fb, 0, :], tmpc[0], tmpc[1], ALU.subtract), reads=[T_TMP[0], T_TMP[1]], writes=[T_P[fb]])
                P.op("dve", TT(tmpc[2], bank(bks[0]), kim, ALU.mult), reads=[TB[bks[0]], T_K[st][1]], writes=[T_TMP[2]])
                P.op("dve", TT(tmpc[3], bank(bks[2]), kre, ALU.mult), reads=[TB[bks[2]], T_K[st][0]], writes=[T_TMP[3]])
                P.op("dve", TT(Pbuf[:, fb, 1, :], tmpc[2], tmpc[3], ALU.add), reads=[T_TMP[2], T_TMP[3]], writes=[T_P[fb]])
            for ts in range(8):
                sl = ts % 2
                P.dma("sp", s_is[sl], lambda g_, ts=ts, sl=sl: [g_.dma_start(
                    out=islot[sl], in_=fi_d[ts].rearrange("p (a b c) -> p a b c", a=2, b=16))],
                      writes=[T_IS[sl]])
                isl = islot[sl]
                for cbl in range(4):
                    cb = 4 * g + cbl
                    bk = nextbank([0, 1, 2, 3, 4, 5, 6, 7])
                    for fc in range(16):
                        for part in range(2):
                            first = (fc == 0 and part == 0)
                            last = (fc == 15 and part == 1)
                            P.op("pe", MM(bank(bk, 256), Pbuf[:, fc, part, cbl * 128:(cbl + 1) * 128], isl[:, part, fc, :], first, last),
                                 reads=[T_P[fc], T_IS[sl]], writes=[TB[bk]], signal=last)
                    xs = x1_bf[:, cb, ts * 256:(ts + 1) * 256]
                    P.op("dve", TT(xs, bank(bk, 256), xs, ALU.mult), reads=[TB[bk], T_X1[cb][ts]], writes=[T_X1[cb][ts]])
        sqb = [ksb[0][i].bitcast(BF16)[:, 0:512] for i in range(2)]
        T_SQ = [T_K[0][0], T_K[0][1]]
        rstd = tmpc[0]
        for tg in range(4):
            bk = nextbank(BK)
            for cb in range(8):
                si = cb % 2
                P.op("act", ACTF(sqb[si], x1_bf[:, cb, tg * 512:(tg + 1) * 512], AF.Square),
                     reads=T_X1[cb][2 * tg:2 * tg + 2], writes=[T_SQ[si]])
                P.op("pe", MM(bank(bk), ones_bf, sqb[si], cb == 0, cb == 7), reads=[T_SQ[si], T_SM], writes=[TB[bk]],
                     signal=True)
            P.op("act", ACTF(rstd, bank(bk), AF.Sqrt, bias=epsc[:, 0:1]), reads=[TB[bk], T_SM], writes=[T_TMP[0]])
            P.op("dve", lambda e: e.reciprocal(out=rstd, in_=rstd), reads=[T_TMP[0]], writes=[T_TMP[0]])
            for cb in range(8):
                xs = x1_bf[:, cb, tg * 512:(tg + 1) * 512]
                P.op("dve", STT(xs, xs, hyg[:, cb:cb + 1], rstd, ALU.mult, ALU.mult),
                     reads=[T_TMP[0], T_SM] + T_X1[cb][2 * tg:2 * tg + 2], writes=T_X1[cb][2 * tg:2 * tg + 2])
        if stop_after == "C":
            dump_and_finish([(R2, 8192)])
            finish()
            return nc

        P.barrier()
        load_xT()
        o = R4
        wq_bf = bfv(o, 3072).rearrange("p (a b) -> p a b", a=16)
        o += 3072
        T_WQ = T()
        s_wq = dsem("wq")
        qT = bfv(o, 1024)
        kT = bfv(o + 1024, 1024)
        vh = bfv(o + 2048, 1024).rearrange("p (a b) -> p a b", a=16)
        o += 3072
        T_Q, T_KK, T_V = [T() for _ in range(4)], [T() for _ in range(4)], [T() for _ in range(4)]
        Wt = bfv(o, 1984)
        o += 1984
        T_WT = T()
        distb = f32v(o, 3968)
        o += 3968
        T_DIST = T()
        NE = 8
        ering = [bfv(o + i * 256, 256) for i in range(NE)]
        o += NE * 256
        T_E = [T() for _ in range(NE)]
        ntmp = [f32v(o + i * 512, 512) for i in range(4)]
        o += 2048
        T_N = [T() for _ in range(4)]
        assert o - R4 <= R4_WORDS, o - R4
        s_d = dsem("dist")
        P.dma("sp", s_d, lambda g: [g.dma_start(out=distb, in_=dist_d)], writes=[T_DIST])
        SB = [0, 1, 2, 3]
        ecount = 0
        for h in range(NH):
            slope = 2.0 ** (-8.0 * (h + 1) / NH)
            P.dma("pool", s_wq, lambda g, h=h: [g.dma_start(out=wq_bf, in_=wqkv_d[h].rearrange("p (a b) -> p a b", a=16))],
                  writes=[T_WQ])
            P.op("act", ACTF(Wt, distb, AF.Exp, scale=-slope), reads=[T_DIST], writes=[T_WT])
            for which, dst, Td in ((0, qT, T_Q), (1, kT, T_KK)):
                for tg in range(4):
                    bk = nextbank(SB)
                    for dc in range(16):
                        P.op("pe", MM(bank(bk), wq_bf[:, dc, which * 128:(which + 1) * 128], xT_bf[:, dc, tg * 512:(tg + 1) * 512],
                                      dc == 0, dc == 15), reads=[T_WQ] + T_R1, writes=[TB[bk]], signal=(dc == 15))
                    P.op("act", lambda e, dst=dst, tg=tg, bk=bk: e.copy(out=dst[:, tg * 512:(tg + 1) * 512], in_=bank(bk)),
                         reads=[TB[bk]], writes=[Td[tg]])
            for t4 in range(4):
                bk = nextbank(SB)
                for i in range(4):
                    tb = t4 * 4 + i
                    for dc in range(16):
                        P.op("pe", MM(bank(bk)[:, i * 128:(i + 1) * 128], xT_bf[:, dc, tb * 128:(tb + 1) * 128], wq_bf[:, dc, 256:384],
                                      dc == 0, dc == 15), reads=[T_WQ] + T_R1, writes=[TB[bk]], signal=(dc == 15 and i == 3))
                P.op("act", lambda e, t4=t4, bk=bk: e.copy(out=vh[:, t4 * 4:t4 * 4 + 4, :],
                                                            in_=bank(bk).rearrange("p (a b) -> p a b", a=4)),
                     reads=[TB[bk]], writes=[T_V[t4]])
            for qg in range(4):
                OB = [4, 5]
                DB = [6, 7]
                tiles = [(kb, c) for kb in range(16) for c in range(2)]
                LAG = 3
                eis = {}

                def front(i, qg=qg):
                    nonlocal ecount
                    kb, c = tiles[i]
                    u0 = 512 * qg - 128 * kb + 1920
                    bk = nextbank(SB)
                    P.op("pe", MM(bank(bk), kT[64 * c:64 * c + 64, kb * 128:(kb + 1) * 128],
                                  qT[64 * c:64 * c + 64, qg * 512:(qg + 1) * 512], True, True),
                         reads=[T_KK[kb // 4], T_Q[qg]], writes=[TB[bk]])
                    ei = ecount % NE
                    ecount += 1
                    eis[i] = ei
                    P.op("act", ACTF(ering[ei], bank(bk), AF.Exp, scale=SCALE), reads=[TB[bk]], writes=[T_E[ei]])
                    P.op("dve", TT(ering[ei], ering[ei], Wt[:, u0:u0 + 512], ALU.mult), reads=[T_WT, T_E[ei]], writes=[T_E[ei]])

                def back(i):
                    kb, c = tiles[i]
                    ei = eis[i]
                    P.op("pe", MM(bank(OB[c]), vh[:, kb, :], ering[ei], kb == 0, kb == 15),
                         reads=[T_V[kb // 4], T_E[ei]], writes=[TB[OB[c]]], signal=(kb == 15))
                    P.op("pe", MM(bank(DB[c]), ones_bf, ering[ei], kb == 0, kb == 15),
                         reads=[T_E[ei], T_SM], writes=[TB[DB[c]]], signal=True)

                for step in range(len(tiles) + LAG):
                    if step < len(tiles):
                        front(step)
                    if step >= LAG:
                        back(step - LAG)
                P.op("dve", lambda e: e.reciprocal(out=ntmp[0], in_=bank(6)), reads=[TB[6]], writes=[T_N[0]])
                P.op("dve", lambda e: e.reciprocal(out=ntmp[1], in_=bank(7)), reads=[TB[7]], writes=[T_N[1]])
                P.op("dve", TT(ntmp[0], bank(4), ntmp[0], ALU.mult), reads=[TB[4], T_N[0]], writes=[T_N[0]])
                P.op("dve", TT(ntmp[1], bank(5), ntmp[1], ALU.mult), reads=[TB[5], T_N[1]], writes=[T_N[1]])
                P.op("dve", STT(ntmp[2], ntmp[1], neglam, ntmp[0], ALU.mult, ALU.add),
                     reads=[T_N[0], T_N[1], T_SM], writes=[T_N[2]])
                ei = ecount % NE
                ecount += 1
                P.op("act", ACTF(ering[ei], ntmp[2], AF.Square), reads=[T_N[2]], writes=[T_E[ei]])
                bk = nextbank(SB)
                P.op("pe", MM(bank(bk), ones_bf, ering[ei], True, True), reads=[T_E[ei], T_SM], writes=[TB[bk]])
                P.op("act", ACTF(ntmp[3], bank(bk), AF.Sqrt, bias=epsc[:, 1:2]), reads=[TB[bk], T_SM], writes=[T_N[3]])
                P.op("dve", lambda e: e.reciprocal(out=ntmp[3], in_=ntmp[3]), reads=[T_N[3]], writes=[T_N[3]])
                P.op("dve", STT(att[:, h, qg * 512:(qg + 1) * 512], ntmp[2], subg[:, 0:1], ntmp[3], ALU.mult, ALU.mult),
                     reads=[T_N[2], T_N[3], T_SM], writes=[T_ATT[h][qg]])
        if stop_after == "D":
            dump_and_finish([(R3, 8192)])
            finish()
            return nc

        P.barrier()
        acc = f32v(R1, 16384).rearrange("p (a b) -> p a b", a=16)
        T_ACC = [[T() for _ in range(2)] for _ in range(16)]
        def cat(mc):
            return att[:, mc, :] if mc < 8 else x1_bf[:, mc - 8, :]
        T_CAT = [[T() for _ in range(4)] for _ in range(16)]
        o = R4
        hT = bfv(o, 2048).rearrange("p (a b) -> p a b", a=4)
        o += 2048
        T_HT = [[T() for _ in range(2)] for _ in range(4)]
        wA = bfv(o, 4096)
        wB = bfv(o + 4096, 4096)
        o += 8192
        T_WA, T_WB = T(), T()
        s_wa, s_wb = dsem("wa"), dsem("wb")
        xr = [f32v(o + i * 512, 512) for i in range(2)]
        o += 1024
        T_XR = [T() for _ in range(2)]
        s_xr = [dsem("xr%d" % i) for i in range(2)]
        sbf = [bfv(o + i * 256, 256) for i in range(4)]
        o += 1024
        T_SB = [T() for _ in range(4)]
        lt = [f32v(o + i * 512, 512) for i in range(5)]
        o += 2560
        T_LT = [T() for _ in range(5)]
        ost = [f32v(o + i * 512, 512) for i in range(2)]
        o += 1024
        T_OST = [T(), T()]
        s_ost = [dsem("ost0"), dsem("ost1")]
        assert o - R4 <= R4_WORDS, o - R4
        wo_v = [wA.rearrange("p (a b) -> p a b", a=16), wB.rearrange("p (a b) -> p a b", a=16)]
        w1_v = wA.rearrange("p (a b) -> p a b", a=16)
        w2_v = wB.rearrange("p (a b) -> p a b", a=4)
        T_W = [T_WA, T_WB]
        s_w = [s_wa, s_wb]
        EB = [0, 1, 2, 3, 4, 5]
        xcount = 0
        scount = 0
        ocount = 0

        def ln_stats(hf, tgl, src_reads_fn):
            nonlocal scount
            for db in range(16):
                a_t = acc[:, db, tgl * 512:(tgl + 1) * 512]
                s0 = scount % 4
                s1 = (scount + 1) % 4
                scount += 2
                P.op("act", lambda e, a_t=a_t, s0=s0: e.copy(out=sbf[s0], in_=a_t), reads=[T_ACC[db][tgl]], writes=[T_SB[s0]])
                P.op("act", ACTF(sbf[s1], a_t, AF.Square), reads=[T_ACC[db][tgl]], writes=[T_SB[s1]])
                P.op("pe", MM(bank(6), ones_bf, sbf[s0], db == 0, db == 15), reads=[T_SB[s0], T_SM], writes=[TB[6]], signal=True)
                P.op("pe", MM(bank(7), ones_bf, sbf[s1], db == 0, db == 15), reads=[T_SB[s1], T_SM], writes=[TB[7]], signal=True)
            P.op("dve", TS(lt[0], bank(6), 1.0 / D, None, ALU.mult), reads=[TB[6]], writes=[T_LT[0]])
            P.op("dve", TT(lt[2], lt[0], lt[0], ALU.mult), reads=[T_LT[0]], writes=[T_LT[2]])
            P.op("dve", STT(lt[1], bank(7), 1.0 / D, lt[2], ALU.mult, ALU.subtract), reads=[TB[7], T_LT[2]], writes=[T_LT[1]])
            P.op("act", ACTF(lt[1], lt[1], AF.Sqrt, bias=epsc[:, 2:3]), reads=[T_LT[1], T_SM], writes=[T_LT[1]])
            P.op("dve", lambda e: e.reciprocal(out=lt[1], in_=lt[1]), reads=[T_LT[1]], writes=[T_LT[1]])

        for hf in range(2):
            for tgl in range(2):
                tg = 2 * hf + tgl
                for db in range(16):
                    dbg_, dbl = db // 4, db % 4
                    wsl = dbg_ % 2
                    if dbl == 0:
                        P.dma("pool", s_w[wsl], lambda g, dbg_=dbg_, wsl=wsl: [g.dma_start(
                            out=wo_v[wsl], in_=wo_d[dbg_].rearrange("p (a b) -> p a b", a=16))], writes=[T_W[wsl]])
                    xi = xcount % 2
                    xcount += 1
                    P.dma("sp", s_xr[xi], lambda g, db=db, tg=tg, xi=xi: [g.dma_start(out=xr[xi], in_=xT_d[db][:, tg * 512:(tg + 1) * 512])],
                          writes=[T_XR[xi]])
                    bk = nextbank(EB)
                    for mc in range(16):
                        P.op("pe", MM(bank(bk), wo_v[wsl][:, mc, dbl * 128:(dbl + 1) * 128], cat(mc)[:, tg * 512:(tg + 1) * 512],
                                      mc == 0, mc == 15), reads=[T_W[wsl], T_CAT[mc][tg]], writes=[TB[bk]], signal=(mc == 15))
                    P.op("dve", STT(acc[:, db, tgl * 512:(tgl + 1) * 512], xr[xi], ALPHA, bank(bk), ALU.mult, ALU.add),
                         reads=[T_XR[xi], TB[bk]], writes=[T_ACC[db][tgl]])
                ln_stats(hf, tgl, None)
                for db in range(16):
                    a_t = acc[:, db, tgl * 512:(tgl + 1) * 512]
                    P.op("dve", TT(lt[2], a_t, lt[0], ALU.subtract), reads=[T_ACC[db][tgl], T_LT[0]], writes=[T_LT[2]])
                    P.op("dve", TT(lt[3], lt[2], lt[1], ALU.mult), reads=[T_LT[2], T_LT[1]], writes=[T_LT[3]])
                    P.op("act", ACTF(cat(db)[:, tg * 512:(tg + 1) * 512], lt[3], AF.Identity,
                                     scale=lnv[:, db:db + 1], bias=lnv[:, 16 + db:17 + db]),
                         reads=[T_LT[3], T_SM] + [T_CAT[m][tg] for m in range(16)], writes=[T_CAT[db][tg]])
                    P.op("act", ACTF(a_t, lt[3], AF.Identity, scale=lnva[:, db:db + 1], bias=lnva[:, 16 + db:17 + db]),
                         reads=[T_LT[3], T_SM], writes=[T_ACC[db][tgl]])
            for ft in range(16):
                P.dma("pool", s_wa, lambda g, ft=ft: [g.dma_start(out=w1_v, in_=w1_d[ft].rearrange("p (a b) -> p a b", a=16))],
                      writes=[T_WA])
                P.dma("pool", s_wb, lambda g, ft=ft: [g.dma_start(out=w2_v, in_=w2_d[ft].rearrange("p (a b) -> p a b", a=4))],
                      writes=[T_WB])
                for fbl in range(4):
                    for tgl in range(2):
                        tg = 2 * hf + tgl
                        bk = nextbank(EB)
                        for dc in range(16):
                            P.op("pe", MM(bank(bk), w1_v[:, dc, fbl * 128:(fbl + 1) * 128], cat(dc)[:, tg * 512:(tg + 1) * 512],
                                          dc == 0, dc == 15), reads=[T_WA, T_CAT[dc][tg]], writes=[TB[bk]], signal=(dc == 15))
                        P.op("act", ACTF(lt[4], bank(bk), AF.Relu), reads=[TB[bk]], writes=[T_LT[4]])
                        P.op("act", ACTF(hT[:, fbl, tgl * 512:(tgl + 1) * 512], lt[4], AF.Square), reads=[T_LT[4]],
                             writes=[T_HT[fbl][tgl]])
                for db in range(16):
                    for tgl in range(2):
                        bk = nextbank(EB)
                        for fc in range(4):
                            P.op("pe", MM(bank(bk), w2_v[:, fc, db * 128:(db + 1) * 128], hT[:, fc, tgl * 512:(tgl + 1) * 512],
                                          fc == 0, fc == 3), reads=[T_WB, T_HT[fc][tgl]], writes=[TB[bk]], signal=(fc == 3))
                        a_t = acc[:, db, tgl * 512:(tgl + 1) * 512]
                        P.op("dve", TT(a_t, bank(bk), a_t, ALU.add), reads=[TB[bk], T_ACC[db][tgl]], writes=[T_ACC[db][tgl]])
            for tgl in range(2):
                tg = 2 * hf + tgl
                ln_stats(hf, tgl, None)
                for db in range(16):
                    a_t = acc[:, db, tgl * 512:(tgl + 1) * 512]
                    P.op("dve", TT(lt[2], a_t, lt[0], ALU.subtract), reads=[T_ACC[db][tgl], T_LT[0]], writes=[T_LT[2]])
                    P.op("dve", TT(lt[3], lt[2], lt[1], ALU.mult), reads=[T_LT[2], T_LT[1]], writes=[T_LT[3]])
                    oi = ocount % 2
                    ocount += 1
                    P.op("act", ACTF(ost[oi], lt[3], AF.Identity, scale=lnv[:, 32 + db:33 + db], bias=lnv[:, 48 + db:49 + db]),
                         reads=[T_LT[3], T_SM], writes=[T_OST[oi]])
                    P.dma("sp", s_ost[oi], lambda g, oi=oi, db=db, tg=tg: [g.dma_start(out=yT_d[db][:, tg * 512:(tg + 1) * 512], in_=ost[oi])],
                          reads=[T_OST[oi]])
        P.barrier()
        for s_ in s_ost:
            P._wait("sp", (s_, P.dma_cnt[s_]))
        finish()
    return nc


_CONST = {}


def _consts():
    if _CONST:
        return _CONST
    bf = ml_dtypes.bfloat16
    f = np.arange(2048, dtype=np.float64)
    t = np.arange(2048, dtype=np.float64)
    theta = 2.0 * np.pi * (f + 0.5) / NFFT
    ang = np.outer(t, theta)
    cf = np.cos(ang)
    sf = -np.sin(ang)
    F = np.stack([cf, sf], 0).reshape(2, 16, 128, 16, 128)
    dft_f = np.ascontiguousarray(F.transpose(3, 2, 0, 1, 4)).reshape(16, 128, 2 * 16 * 128).astype(bf)
    ci = (2.0 / NFFT) * cf.T
    si = (2.0 / NFFT) * sf.T
    G = np.stack([ci, si], 0).reshape(2, 16, 128, 8, 256)
    dft_i = np.ascontiguousarray(G.transpose(3, 2, 0, 1, 4)).reshape(8, 128, 2 * 16 * 256).astype(bf)
    tt = np.linspace(0.0, 1.0, L, dtype=np.float32)[:, None]
    bands = 16
    w = (2.0 * np.pi * np.arange(L, dtype=np.float32)[:, None] / L).astype(np.float32)
    fr = np.linspace(1e-4, bands - 1, bands, dtype=np.float32)[None, :]
    z = np.concatenate([tt, np.cos(fr * w), -np.sin(fr * w)], axis=-1).astype(np.float32)
    zT = np.ascontiguousarray(z.T)
    tl = np.ascontiguousarray(tt[:, 0].reshape(16, 128).T)
    max_decay = math.log(1e-2) / 0.3
    min_decay = math.log(1e-2) / 1.5
    deltas = np.linspace(min_decay, max_decay, C, dtype=np.float32)
    negdelta = np.ascontiguousarray(np.broadcast_to(-np.abs(deltas)[None, :], (128, C))).astype(np.float32)
    u = np.arange(3968, dtype=np.float32)[None, :]
    p = np.arange(128, dtype=np.float32)[:, None]
    dist = np.abs(u - 1920.0 - p).astype(np.float32)
    ident = np.eye(128, dtype=np.float32).astype(bf)
    _CONST.update(dft_f=dft_f, dft_i=dft_i, zT=zT, tl=tl, negdelta=negdelta, dist=dist, ident=ident)
    return _CONST


def _prep_weights(inp):
    f32 = np.float32
    w_in = np.asarray(inp["w_in"], f32)[0]
    wr = w_in.reshape(16, 128, 6144)
    hcols = wr[:, :, 3 * A:].reshape(16, 128, 3, 8, 128)
    wh = np.ascontiguousarray(hcols.transpose(3, 1, 0, 2, 4)).reshape(8, 128, 16 * 384)
    q = wr[:, :, 0:A].reshape(16, 128, 8, 128)
    k = wr[:, :, A:2 * A].reshape(16, 128, 8, 128)
    v = wr[:, :, 2 * A:3 * A].reshape(16, 128, 8, 128)
    qkv = np.stack([q, k, v], 3)
    wqkv = np.ascontiguousarray(qkv.transpose(2, 1, 0, 3, 4)).reshape(8, 128, 16 * 384)
    w_out = np.asarray(inp["w_out"], f32)[0].reshape(16, 128, 4, 512)
    wo = np.ascontiguousarray(w_out.transpose(2, 1, 0, 3)).reshape(4, 128, 16 * 512)
    w_ff1 = np.asarray(inp["w_ff1"], f32)[0].reshape(16, 128, 16, 512)
    w1 = np.ascontiguousarray(w_ff1.transpose(2, 1, 0, 3)).reshape(16, 128, 16 * 512)
    w_ff2 = np.asarray(inp["w_ff2"], f32)[0].reshape(16, 4, 128, 2048)
    w2 = np.ascontiguousarray(w_ff2.transpose(0, 2, 1, 3)).reshape(16, 128, 4 * 2048)
    conv_w = np.asarray(inp["conv_w"], f32)[0]
    conv_b = np.asarray(inp["conv_b"], f32)[0]
    cwb = np.concatenate([conv_w, conv_b[None, :]], 0).reshape(4, 3, 8, 128)
    cw = np.ascontiguousarray(cwb.transpose(3, 2, 1, 0)).reshape(128, 96)
    fvec = np.stack([np.asarray(inp["filt_b1"], f32)[0], np.asarray(inp["filt_freq"], f32)[0],
                     np.asarray(inp["filt_b2"], f32)[0]], 1)
    lam = np.concatenate([np.asarray(inp[n], f32)[0] for n in ("lambda_q1", "lambda_k1", "lambda_q2", "lambda_k2")])
    lamin = np.ascontiguousarray(np.broadcast_to(lam[None, :], (128, 256)))
    lnv = np.concatenate([np.asarray(inp[n], f32)[0].reshape(16, 128).T for n in ("ln1_g", "ln1_b", "ln2_g", "ln2_b")], 1)
    bf = ml_dtypes.bfloat16
    sm = np.zeros((128, SM_USED), np.float32)

    def put(name, arr):
        o_, n_ = SMO[name]
        arr = np.asarray(arr, np.float32)
        sm[:arr.shape[0], o_:o_ + arr.shape[1]] = arr

    put("ident", np.eye(128, dtype=np.float32).astype(bf).view(np.float32))
    put("ones", np.ones((128, 128), np.float32).astype(bf).view(np.float32))
    put("cw", cw)
    put("fw1", np.asarray(inp["filt_w1"], f32)[0])
    put("fw2", np.asarray(inp["filt_w2"], f32)[0])
    put("fvec", fvec)
    put("tl", _consts()["tl"])
    put("hyg", np.asarray(inp["hyena_gain"], f32)[0].reshape(8, 128).T)
    put("lamin", lamin)
    put("subg", np.asarray(inp["subln_g"], f32)[0].reshape(128, 1))
    put("lnv", lnv)
    put("epsc", np.broadcast_to(np.array([[1024.0 * EPS, 128.0 * EPS, EPS, 0.0]], np.float32), (128, 4)))
    csm = np.zeros((128, 4096), np.float32)
    csm[:64, 0:2048] = np.asarray(inp["filt_w3"], f32)[0]
    csm[:, 2048:3072] = _consts()["negdelta"]
    csm[0, 3072:4096] = np.asarray(inp["hyena_skip"], f32)[0]
    out = dict(wh=wh, wqkv=wqkv, wo=wo, w1=w1, w2=w2, smalls=sm, csm=csm)
    return out


def make_in_maps(inp):
    c = _consts()
    shared = _prep_weights(inp)
    zTp = np.zeros((128, 2048), np.float32)
    zTp[:33] = c["zT"]
    shared.update(dft_f=c["dft_f"], dft_i=c["dft_i"], zTp=zTp, dist=c["dist"])
    x = np.asarray(inp["x"], np.float32)
    maps = []
    for b in range(8):
        xT = np.ascontiguousarray(x[b].T).reshape(16, 128, 2048)
        m = dict(shared)
        m["xT"] = xT
        maps.append(m)
    return maps


def kernel(**inputs):
    nc = build_nc()
    maps = make_in_maps(inputs)
    res = run_bass_kernel_spmd(nc, maps, core_ids=list(range(8)))
    out = np.empty((8, S, D), np.float32)
    for b in range(8):
        yT = np.asarray(res.results[b]["yT"]).reshape(D, S)
        out[b] = yT.T
    return out
```
